# Optimizing a Trainium2 kernel written in Bass

```python
import jax, jax.numpy as jnp
from jax import lax
import numpy as np

D_MODEL = 1024
BATCH = 4
SEQ = 4096
DEPTH = 2

N_MEM = 256
ATT_GROUPS = ((128, 1), (512, 4), (2048, 16))
N_ATT_GROUPS = len(ATT_GROUPS)
ATT_HEADS = 4
ATT_HEAD_DIM = 128
ATT_WIDTH = ATT_HEADS * ATT_HEAD_DIM
Q_BLOCK = 128
HGRN_EXPAND = 128
HGRN_WIDTH = D_MODEL
HGRN_HEADS = HGRN_WIDTH // HGRN_EXPAND
HGRN_CHUNK = 64
D_FF = 2816
X_HEADS = 4
X_HEAD_DIM = D_MODEL // X_HEADS
ROPE_THETA = 10000.0
EPS = 1e-6
IN_SIZES = (N_ATT_GROUPS * ATT_WIDTH,) * 3 + (HGRN_WIDTH,) * 4 + (D_MODEL,) * 2
N_IN = sum(IN_SIZES)

kernel_name = "hybrid_dilated_attn_hgrn2_macaron_block"


def rms_norm(x, g):
    xf = x.astype(jnp.float32)
    y = xf * lax.rsqrt(jnp.mean(xf * xf, axis=-1, keepdims=True) + EPS)
    return (y * g.astype(jnp.float32)).astype(x.dtype)


def swiglu(h, w_gu, w_down):
    gate, up = jnp.split(h @ w_gu, 2, axis=-1)
    return (jax.nn.silu(gate) * up) @ w_down


def rope_tables(positions, dim, dtype):
    inv_freq = ROPE_THETA ** (-jnp.arange(0, dim, 2, dtype=jnp.float32) / dim)
    ang = positions.astype(jnp.float32)[..., None] * inv_freq
    cos = jnp.cos(ang)[:, :, None, None, :].astype(dtype)
    sin = jnp.sin(ang)[:, :, None, None, :].astype(dtype)
    return cos, sin


def apply_rope(x, cos, sin):
    x1, x2 = jnp.split(x, 2, axis=-1)
    return jnp.concatenate([x1 * cos - x2 * sin, x2 * cos + x1 * sin], axis=-1)


def dilated_attention(q, k, v):
    B, T, G, H, dh = q.shape
    n_blk = T // Q_BLOCK
    scale = dh ** -0.5
    k_groups = [k[:, :, gi] for gi in range(G)]
    v_groups = [v[:, :, gi] for gi in range(G)]

    def block(b):
        t0 = b * Q_BLOCK
        tq = t0 + jnp.arange(Q_BLOCK)
        q_blk = lax.dynamic_slice_in_dim(q, t0, Q_BLOCK, axis=1)
        outs, lses = [], []
        for gi, (window, dil) in enumerate(ATT_GROUPS):
            j = jnp.arange(window // dil + 1)
            idx = tq[:, None] - dil * j[None, :]
            valid = idx >= 0
            idx = jnp.maximum(idx, 0)
            k_sel = k_groups[gi][:, idx]
            v_sel = v_groups[gi][:, idx]
            s = jnp.einsum('bqhd,bqjhd->bqhj', q_blk[:, :, gi], k_sel,
                           preferred_element_type=jnp.float32) * scale
            s = jnp.where(valid[None, :, None, :], s, -jnp.inf)
            m = jnp.max(s, axis=-1, keepdims=True)
            p = jnp.exp(s - m)
            den = jnp.sum(p, axis=-1)
            o = jnp.einsum('bqhj,bqjhd->bqhd', p, v_sel.astype(jnp.float32)) / den[..., None]
            outs.append(o)
            lses.append(m[..., 0] + jnp.log(den))
        w = jax.nn.softmax(jnp.stack(lses, axis=-1), axis=-1)
        o = jnp.einsum('bqhgd,bqhg->bqhd', jnp.stack(outs, axis=-2), w)
        return o.astype(q.dtype)

    o = lax.map(block, jnp.arange(n_blk))
    return o.transpose(1, 0, 2, 3, 4).reshape(B, T, H * dh)


def hgrn2_recurrence(q, f_logit, i, lb):
    B, T, H, N = q.shape
    C = HGRN_CHUNK
    NC = T // C
    lb = lb.reshape(H, N).astype(jnp.float32)
    f = lb + (1.0 - lb) * jax.nn.sigmoid(f_logit.astype(jnp.float32))
    g = jnp.log(f)
    k = 1.0 - f
    qs = q.astype(jnp.float32) * (N ** -0.5)

    def to_chunks(a):
        return a.reshape(B, NC, C, H, a.shape[-1]).transpose(1, 0, 3, 2, 4)

    causal = jnp.tril(jnp.ones((C, C), dtype=bool))

    def step(S, inp):
        qc, kc, vc, gc = inp
        b = jnp.cumsum(gc, axis=2)
        diff = b[:, :, :, None, :] - b[:, :, None, :, :]
        decay = jnp.exp(jnp.where(causal[:, :, None], diff, -jnp.inf))
        A = jnp.einsum('bhtn,bhsn,bhtsn->bhts', qc, kc, decay)
        o = (jnp.einsum('bhts,bhsv->bhtv', A, vc)
             + jnp.einsum('bhtn,bhnv->bhtv', qc * jnp.exp(b), S))
        b_last = b[:, :, -1:, :]
        S = (jnp.exp(b_last[:, :, 0, :])[..., None] * S
             + jnp.einsum('bhsn,bhsv->bhnv', kc * jnp.exp(b_last - b), vc))
        return S, o

    S0 = jnp.zeros((B, H, N, i.shape[-1]), jnp.float32)
    _, o = lax.scan(step, S0, (to_chunks(qs), to_chunks(k),
                               to_chunks(i.astype(jnp.float32)), to_chunks(g)))
    return o.transpose(1, 0, 3, 2, 4).reshape(B, T, H, -1)


def cross_attention(h, mem_n, wq, wkv, wo):
    B, T, _ = h.shape
    M = mem_n.shape[1]
    q = (h @ wq).reshape(B, T, X_HEADS, X_HEAD_DIM)
    k, v = jnp.split(mem_n @ wkv, 2, axis=-1)
    k = k.reshape(B, M, X_HEADS, X_HEAD_DIM)
    v = v.reshape(B, M, X_HEADS, X_HEAD_DIM)
    s = jnp.einsum('bthd,bmhd->bhtm', q, k, preferred_element_type=jnp.float32) * (X_HEAD_DIM ** -0.5)
    p = jax.nn.softmax(s, axis=-1)
    o = jnp.einsum('bhtm,bmhd->bthd', p, v.astype(jnp.float32)).astype(h.dtype)
    return o.reshape(B, T, X_HEADS * X_HEAD_DIM) @ wo


def setup_inputs(seed: int = 0) -> dict:
    key = jax.random.key(seed)
    ks = iter(jax.random.split(key, 32))
    f32 = jnp.float32

    def w(shape, fan_in):
        return jax.random.normal(next(ks), shape, f32) * (fan_in ** -0.5)

    def gain(shape):
        return 1.0 + 0.02 * jax.random.normal(next(ks), shape, f32)

    L = DEPTH
    return {
        "x": jax.random.normal(next(ks), (BATCH, SEQ, D_MODEL), f32),
        "mem": jax.random.normal(next(ks), (BATCH, N_MEM, D_MODEL), f32),
        "positions": jnp.broadcast_to(jnp.arange(SEQ, dtype=jnp.int32), (BATCH, SEQ)),
        "ffn1_norm": gain((L, D_MODEL)),
        "ffn1_w_gu": w((L, D_MODEL, 2 * D_FF), D_MODEL),
        "ffn1_w_down": w((L, D_FF, D_MODEL), D_FF),
        "mix_norm": gain((L, D_MODEL)),
        "w_in": w((L, D_MODEL, N_IN), D_MODEL),
        "hgrn_lower_bounds": 0.1 * jax.random.normal(next(ks), (L, HGRN_WIDTH), f32),
        "hgrn_head_norm": gain((L, HGRN_WIDTH)),
        "w_att_branch": w((L, ATT_WIDTH, D_MODEL), ATT_WIDTH),
        "w_hgrn_branch": w((L, HGRN_WIDTH, D_MODEL), HGRN_WIDTH),
        "w_mix_out": w((L, D_MODEL, D_MODEL), D_MODEL),
        "xattn_norm": gain((L, D_MODEL)),
        "mem_norm": gain((L, D_MODEL)),
        "xattn_wq": w((L, D_MODEL, X_HEADS * X_HEAD_DIM), D_MODEL),
        "xattn_wkv": w((L, D_MODEL, 2 * X_HEADS * X_HEAD_DIM), D_MODEL),
        "xattn_wo": w((L, X_HEADS * X_HEAD_DIM, D_MODEL), X_HEADS * X_HEAD_DIM),
        "ffn2_norm": gain((L, D_MODEL)),
        "ffn2_w_gu": w((L, D_MODEL, 2 * D_FF), D_MODEL),
        "ffn2_w_down": w((L, D_FF, D_MODEL), D_FF),
        "final_norm": gain((D_MODEL,)),
    }


def reference(x, mem, positions, ffn1_norm, ffn1_w_gu, ffn1_w_down, mix_norm, w_in,
              hgrn_lower_bounds, hgrn_head_norm, w_att_branch, w_hgrn_branch, w_mix_out,
              xattn_norm, mem_norm, xattn_wq, xattn_wkv, xattn_wo,
              ffn2_norm, ffn2_w_gu, ffn2_w_down, final_norm):
    B, T, _ = x.shape
    cos, sin = rope_tables(positions, ATT_HEAD_DIM, x.dtype)
    lb_p = jax.nn.softmax(hgrn_lower_bounds.astype(jnp.float32), axis=0)
    lb_all = jnp.cumsum(lb_p, axis=0) - lb_p[0]
    split_at = [sum(IN_SIZES[:n]) for n in range(1, len(IN_SIZES))]

    for l in range(DEPTH):
        x = x + 0.5 * swiglu(rms_norm(x, ffn1_norm[l]), ffn1_w_gu[l], ffn1_w_down[l])

        h = rms_norm(x, mix_norm[l])
        z = h @ w_in[l]
        q_a, k_a, v_a, q_b, f_b, i_b, og_b, gate_a, gate_b = jnp.split(z, split_at, axis=-1)
        att_shape = (B, T, N_ATT_GROUPS, ATT_HEADS, ATT_HEAD_DIM)
        q_a = apply_rope(q_a.reshape(att_shape), cos, sin)
        k_a = apply_rope(k_a.reshape(att_shape), cos, sin)
        v_a = v_a.reshape(att_shape)
        y_a = dilated_attention(q_a, k_a, v_a) @ w_att_branch[l]

        hg_shape = (B, T, HGRN_HEADS, HGRN_EXPAND)
        o_b = hgrn2_recurrence(q_b.reshape(hg_shape), f_b.reshape(hg_shape),
                               i_b.reshape(hg_shape), lb_all[l])
        o_b = rms_norm(o_b, hgrn_head_norm[l].reshape(HGRN_HEADS, HGRN_EXPAND))
        o_b = o_b.reshape(B, T, HGRN_WIDTH).astype(x.dtype) * jax.nn.silu(og_b)
        y_b = o_b @ w_hgrn_branch[l]

        merged = jax.nn.sigmoid(gate_a) * y_a + jax.nn.sigmoid(gate_b) * y_b
        x = x + merged @ w_mix_out[l]

        x = x + cross_attention(rms_norm(x, xattn_norm[l]), rms_norm(mem, mem_norm[l]),
                                xattn_wq[l], xattn_wkv[l], xattn_wo[l])

        x = x + 0.5 * swiglu(rms_norm(x, ffn2_norm[l]), ffn2_w_gu[l], ffn2_w_down[l])

    return rms_norm(x, final_norm)
```

```python
import math
import numpy as np
import ml_dtypes
from contextlib import ExitStack
import concourse.bass as bass
import concourse.mybir as mybir
from concourse.bass_utils import run_bass_kernel_spmd

F32 = mybir.dt.float32
BF16 = mybir.dt.bfloat16
I32 = mybir.dt.int32
AF = mybir.ActivationFunctionType
ALU = mybir.AluOpType

PE, ACT, DVE, POOL, SP = "pe", "act", "dve", "pool", "sp"
SEM_CAP = 30000
D = 1024
DFF = 2816
NIN = 10752
TT = 256
NS = TT // 128
NMEM = 256
EPS = 1e-6
GROUPS = ((128, 1), (512, 4), (2048, 16))
DMAX = (1, 4, 16)
MBASE = (0, 2, 7)
NMASK = 24
NBLK = (DMAX[0] + NS, DMAX[1] + NS, DMAX[2] + NS)


class Buf:
    __slots__ = ("name", "writers", "readers", "prev_readers", "dma_sem", "dma_cnt")

    def __init__(self, name):
        self.name = name
        self.writers = []
        self.readers = []
        self.prev_readers = []
        self.dma_sem = None
        self.dma_cnt = 0


class Op:
    __slots__ = ("eng", "fn", "deps", "is_dma", "sig", "ev", "dsem")

    def __init__(self, eng, fn, is_dma):
        self.eng = eng
        self.fn = fn
        self.deps = []
        self.is_dma = is_dma
        self.sig = False
        self.ev = None
        self.dsem = None


class Sched:
    def __init__(self):
        self.ops = {PE: [], ACT: [], DVE: [], POOL: [], SP: []}
        self.all_ops = []
        self.dma_keys = []

    def _add(self, eng, fn, reads, writes, partial, is_dma, dma_key=None):
        op = Op(eng, fn, is_dma)
        deps = op.deps
        reads = list(dict.fromkeys(reads))
        writes = list(dict.fromkeys(writes))
        for b in reads:
            deps.extend(b.writers)
            b.readers.append(op)
        for b in writes:
            rd = [r for r in b.readers if r is not op]
            if rd:
                deps.extend(rd)
                deps.extend(b.writers)
                b.prev_readers = rd
                b.readers = [op] if len(rd) != len(b.readers) else []
                b.writers = [op]
            else:
                deps.extend(b.prev_readers)
                if not partial:
                    deps.extend(b.writers)
                    b.writers = [op]
                else:
                    b.writers.append(op)
        if is_dma:
            key = dma_key
            if key is None:
                key = writes[0] if writes else reads[0]
            op.dsem = key
            if key.dma_sem is None:
                key.dma_sem = True
                self.dma_keys.append(key)
        self.ops[eng].append(op)
        self.all_ops.append(op)
        return op

    def op(self, eng, fn, reads=(), writes=(), partial=False):
        return self._add(eng, fn, list(reads), list(writes), partial, False)

    def dma(self, eng, fn, reads=(), writes=(), partial=True, key=None):
        return self._add(eng, fn, list(reads), list(writes), partial, True, key)

    def finalize(self, nc, stack):
        for op in self.all_ops:
            for d in op.deps:
                if d.is_dma:
                    continue
                if d.eng == PE and op.eng == PE and not op.is_dma:
                    continue
                d.sig = True
        nsem = 0
        for eng in (PE, ACT, DVE, POOL):
            cnt = 0
            n = 0
            cur = None
            for op in self.ops[eng]:
                if op.is_dma or not op.sig:
                    continue
                if cur is None or cnt >= SEM_CAP:
                    cur = stack.enter_context(nc.semaphore(f"s_{eng}_{n}"))
                    n += 1
                    cnt = 0
                    nsem += 1
                cnt += 1
                op.ev = (cur, cnt)
        for key in self.dma_keys:
            key.dma_sem = stack.enter_context(nc.semaphore(f"d_{key.name}"))
            key.dma_cnt = 0
            nsem += 1
        for op in self.all_ops:
            if op.is_dma:
                k = op.dsem
                k.dma_cnt += 16
                op.ev = (k.dma_sem, k.dma_cnt)
        self.nsem = nsem

    def emit_engine(self, eng_name, eng):
        waited = {}
        for op in self.ops[eng_name]:
            need = {}
            for d in op.deps:
                if d.eng == PE and eng_name == PE and not d.is_dma and not op.is_dma:
                    continue
                sem, val = d.ev
                k = id(sem)
                if waited.get(k, 0) >= val:
                    continue
                if k not in need or need[k][1] < val:
                    need[k] = (sem, val)
            for k, (sem, val) in need.items():
                eng.wait_ge(sem, val)
                waited[k] = val
            ins = op.fn(eng)
            if op.is_dma:
                ins.then_inc(op.ev[0], 16)
            elif op.sig:
                ins.then_inc(op.ev[0], 1)


def _host_consts():
    c = {}
    c["c_ident"] = np.eye(128, dtype=np.float32).astype(ml_dtypes.bfloat16)
    perm = np.zeros((128, 128), np.float32)
    for dp in range(128):
        perm[(dp + 64) % 128, dp] = 1.0
    c["c_perm"] = perm.astype(ml_dtypes.bfloat16)
    c["c_ones"] = np.ones((128, 128), np.float32).astype(ml_dtypes.bfloat16)
    masks = np.zeros((128, NMASK, 128), np.float32)
    jj = np.arange(128)[:, None]
    ii = np.arange(128)[None, :]
    for g, (win, dil) in enumerate(GROUPS):
        for dl in range(DMAX[g] + 1):
            diff = 128 * dl + ii - jj
            ok = (diff >= 0) & (diff <= win) & (diff % dil == 0)
            masks[:, MBASE[g] + dl, :] = ok
    c["c_mask"] = masks.astype(ml_dtypes.bfloat16)
    caus = ((jj <= ii) & (jj // 64 == ii // 64)).astype(np.float32)
    c["c_caus"] = caus.astype(ml_dtypes.bfloat16)
    sc = np.ones((128, TT), np.float32)
    sc[:, ::64] = 0.0
    c["c_scan"] = sc
    fr = np.zeros((128, 2), np.float32)
    inv = 10000.0 ** (-np.arange(0, 128, 2, dtype=np.float32) / 128.0)
    fr[:, 0] = np.concatenate([inv, inv])
    fr[:64, 1] = -1.0
    fr[64:, 1] = 1.0
    c["c_freq"] = fr
    return c


CONST_SPECS = [("c_ident", [128, 128], BF16), ("c_perm", [128, 128], BF16), ("c_ones", [128, 128], BF16),
               ("c_mask", [128, NMASK, 128], BF16), ("c_caus", [128, 128], BF16),
               ("c_scan", [128, TT], F32), ("c_freq", [128, 2], F32)]

VC = 56


def build_program(T, NL):
    NT = T // TT
    nc = bass.Bass("TRN2", target_bir_lowering=False)
    S = Sched()
    dr = {}

    def din(name, shape, dt):
        dr[name] = nc.dram_tensor(name, shape, dt, kind="ExternalInput").ap()
        return dr[name]

    x_in = din("x", [T, D], F32)
    mem_in = din("mem", [NMEM, D], F32)
    pos_in = din("posb", [128, T], I32)
    vecs_in = din("vecs", [128, NL * VC], F32)
    fin_in = din("fing", [128, D], F32)
    for n, sh, dt in CONST_SPECS:
        din(n, sh, dt)
    W = {}
    for n, r, c in [("ffn1_w_gu", D, 2 * DFF), ("ffn1_w_down", DFF, D), ("w_in", D, NIN),
                    ("w_att_branch", 512, D), ("w_hgrn_branch", D, D), ("w_mix_out", D, D),
                    ("xattn_wq", D, D), ("xattn_wkv", D, 2 * D), ("xattn_wo", D, D),
                    ("ffn2_w_gu", D, 2 * DFF), ("ffn2_w_down", DFF, D)]:
        W[n] = din(n, [NL, r, c], F32)
    out = nc.dram_tensor("out", [T, D], F32, kind="ExternalOutput").ap()
    xscr = nc.dram_tensor("xscr", [T, D], F32).ap()
    ropec = nc.dram_tensor("ropec", [128, T], F32).ap()
    ropes = nc.dram_tensor("ropes", [128, T], F32).ap()
    b_xscr = Buf("xscr")
    b_rope = Buf("ropescr")

    st = ExitStack()
    with st:
        def sb(name, shape, dt):
            return st.enter_context(nc.sbuf_tensor("sb_" + name, shape, dt))

        cst = {}
        b_const = Buf("const")
        for n, sh, dt in CONST_SPECS:
            cst[n] = sb("s" + n, sh, dt)
        vecs = sb("vecs", [128, NL * VC], F32)
        fing = sb("fing", [128, D], F32)
        epsc = sb("epsc", [128, 1], F32)
        lbc = sb("lbc", [128, NL * 8], F32)
        omlc = sb("omlc", [128, NL * 8], F32)
        x_sb = sb("x_sb", [128, NS, D], F32)
        b_x = [Buf(f"x{s}") for s in range(NS)]
        xn = [sb(f"xn{s}", [128, D], BF16) for s in range(NS)]
        b_xn = [Buf(f"xn{s}") for s in range(NS)]
        b_junk = Buf("junk")
        ss = sb("ss", [128, 4], F32)
        rs = sb("rs", [128, 4], F32)
        rstd = sb("rstd", [128, 4], F32)
        b_ss, b_rs, b_rstd = Buf("ss"), Buf("rs"), Buf("rstd")
        hT = sb("hT", [128, 8, TT], BF16)
        b_hT = Buf("hT")
        NSLOT = 4
        wsl = [sb(f"w{i}", [128, 8, 512], BF16) for i in range(NSLOT)]
        b_w = [Buf(f"w{i}") for i in range(NSLOT)]
        kc = [sb(f"kc{g}", [128, NBLK[g], 4, 128], BF16) for g in range(3)]
        vc = [sb(f"vc{g}", [128, NBLK[g], 4, 128], BF16) for g in range(3)]
        b_kc = [[Buf(f"kc{g}_{i}") for i in range(NBLK[g])] for g in range(3)]
        b_vc = [[Buf(f"vc{g}_{i}") for i in range(NBLK[g])] for g in range(3)]
        Sst = sb("Sst", [128, 8, 128], F32)
        Stb = sb("Stb", [128, 8, 128], BF16)
        b_S = [Buf(f"S{h}") for h in range(8)]
        b_St = [Buf(f"St{h}") for h in range(8)]
        kxT = sb("kxT", [128, 8, NMEM], BF16)
        vx = sb("vx", [128, 2, D], BF16)
        b_kxT, b_vx = Buf("kxT"), Buf("vx")
        ropc = sb("ropc", [128, TT], F32)
        rops = sb("rops", [128, TT], F32)
        b_ropt = Buf("ropt")
        act = sb("act", [128, 22, TT], BF16)
        b_act = Buf("act")
        qa = sb("qa", [128, 12, TT], BF16)
        junk = qa[:, 0:4, :].rearrange("p a b -> p (a b)")
        b_qa = [Buf(f"qa{i}") for i in range(12)]
        raw = [sb(f"raw{i}", [128, TT], BF16) for i in range(2)]
        b_raw = [Buf(f"raw{i}") for i in range(2)]
        rt1 = [sb(f"rt1_{i}", [128, TT], F32) for i in range(2)]
        rt2 = [sb(f"rt2_{i}", [128, TT], F32) for i in range(2)]
        b_rt1 = [Buf(f"rt1_{i}") for i in range(2)]
        b_rt2 = [Buf(f"rt2_{i}") for i in range(2)]
        pe_ = [sb(f"pe{i}", [128, 512], BF16) for i in range(2)]
        pm_ = [sb(f"pm{i}", [128, 512], BF16) for i in range(2)]
        b_pe = [Buf(f"pe{i}") for i in range(2)]
        b_pm = [Buf(f"pm{i}") for i in range(2)]
        rec = [sb(f"rec{i}", [128, TT], F32) for i in range(2)]
        b_rec = [Buf(f"rec{i}") for i in range(2)]
        oa = sb("oa", [128, 4, TT], BF16)
        b_oa = Buf("oa")
        Tt = [sb(f"T{i}", [128, TT], F32) for i in range(9)]
        b_T = [Buf(f"T{i}") for i in range(9)]
        klb = sb("klb", [128, TT], BF16)
        b_klb = Buf("klb")
        qd = sb("qd", [128, 8, TT], BF16)
        kd = sb("kd", [128, 8, TT], BF16)
        b_qd = [Buf(f"qd{h}") for h in range(8)]
        b_kd = [Buf(f"kd{h}") for h in range(8)]
        klT = sb("klT", [128, 8, NS, 128], BF16)
        b_klT = [Buf(f"klT{h}") for h in range(8)]
        dec = sb("dec", [128, 8, 4], F32)
        emid = sb("emid", [128, 8, 4], F32)
        el = sb("el", [128, 8, 4], F32)
        b_dec = [Buf(f"dec{h}") for h in range(8)]
        iTM = sb("iTM", [128, NS, D], BF16)
        b_iTM = Buf("iTM")
        ATb = [sb(f"AT{i}", [128, 128], BF16) for i in range(2)]
        b_AT = [Buf(f"AT{i}") for i in range(2)]
        oT = sb("oT", [128, 8, TT], F32)
        b_oT = [Buf(f"oT{h}") for h in range(8)]
        sqb = sb("sqb", [128, TT], BF16)
        b_sqb = Buf("sqb")
        ob = sb("ob", [128, 8, TT], BF16)
        b_ob = Buf("ob")
        ya = sb("ya", [128, TT], F32)
        yb = sb("yb", [128, TT], F32)
        b_ya, b_yb = Buf("ya"), Buf("yb")
        merged = sb("merged", [128, 8, TT], BF16)
        b_merged = Buf("merged")
        qx = qd
        b_qx = Buf("qx")
        pT = sb("pT", [128, 2, TT], BF16)
        b_pT = Buf("pT")
        ox = kd
        b_ox = Buf("ox")
        o_sb = act[:, 0:16, :].rearrange("p a b -> p (a b)").bitcast(F32).rearrange("p (s d) -> p s d", d=D)
        b_o = [b_act, b_act]
        NB = 6
        pbank = [st.enter_context(nc.psum_tensor(f"pb{i}", [128, 512], F32)) for i in range(NB)]
        b_pb = [Buf(f"pb{i}") for i in range(NB)]
        ptb = [st.enter_context(nc.psum_tensor(f"ptb{i}", [128, 1024], BF16)) for i in range(2)]
        b_ptb = [Buf(f"ptb{i}") for i in range(2)]
        ctr = {"bank": 0, "tb": 0, "w": 0, "raw": 0, "pe": 0, "rec": 0, "AT": 0}

        held = set()

        def bank(hold=False):
            while True:
                i = ctr["bank"] % NB
                ctr["bank"] += 1
                if i not in held:
                    break
            if hold:
                held.add(i)
            return pbank[i], b_pb[i]

        def release(bb):
            held.discard(b_pb.index(bb))

        def tbank():
            i = ctr["tb"] % 2
            ctr["tb"] += 1
            return ptb[i], b_ptb[i]

        def rot(name, n):
            i = ctr[name] % n
            ctr[name] += 1
            return i

        def mm(o, l, r, start, stop, reads, writes):
            S.op(PE, lambda e: e.matmul(o, lhsT=l, rhs=r, start=start, stop=stop), reads=reads, writes=writes, partial=True)

        def tr(o, i, reads, writes):
            idt = cst["c_ident"]
            S.op(PE, lambda e: e.transpose(o, i, idt[:, :]), reads=reads, writes=writes, partial=True)

        def act_(o, i, func, reads, writes, scale=1.0, bias=None, accum=None, partial=False):
            kw = {}
            if bias is not None:
                kw["bias"] = bias
            if accum is not None:
                kw["accum_out"] = accum
            S.op(ACT, lambda e: e.activation(out=o, in_=i, func=func, scale=scale, **kw), reads=reads, writes=writes, partial=partial)

        def tt(o, a, b, op, reads, writes, partial=False, eng=DVE):
            S.op(eng, lambda e: e.tensor_tensor(out=o, in0=a, in1=b, op=op), reads=reads, writes=writes, partial=partial)

        def ts(o, a, s1, s2, op0, op1, reads, writes, partial=False):
            if s2 is None:
                S.op(DVE, lambda e: e.tensor_scalar(out=o, in0=a, scalar1=s1, scalar2=None, op0=op0), reads=reads, writes=writes, partial=partial)
            else:
                S.op(DVE, lambda e: e.tensor_scalar(out=o, in0=a, scalar1=s1, scalar2=s2, op0=op0, op1=op1), reads=reads, writes=writes, partial=partial)

        def stt(o, a, sc, b, op0, op1, reads, writes, partial=False):
            S.op(DVE, lambda e: e.scalar_tensor_tensor(out=o, in0=a, scalar=sc, in1=b, op0=op0, op1=op1), reads=reads, writes=writes, partial=partial)

        def wtile(wap, r0, nk, c0):
            i = rot("w", NSLOT)
            src = wap[r0:r0 + nk * 128, c0:c0 + 512].rearrange("(k p) c -> p k c", p=128)
            dst = wsl[i][:, 0:nk, :]
            S.dma(POOL, lambda e: e.dma_start(out=dst, in_=src), writes=[b_w[i]], partial=False)
            return wsl[i], b_w[i]

        for n, sh, dt in CONST_SPECS:
            t_ = cst[n]
            full = t_[:, :, :] if len(sh) == 3 else t_[:, :]
            srcf = dr[n][:, :, :] if len(sh) == 3 else dr[n][:, :]
            S.dma(SP, lambda e, o=full, i=srcf: e.dma_start(out=o, in_=i), writes=[b_const])
        S.dma(SP, lambda e: e.dma_start(out=vecs[:, :], in_=vecs_in[:, :]), writes=[b_const])
        S.dma(SP, lambda e: e.dma_start(out=fing[:, :], in_=fin_in[:, :]), writes=[b_const])
        b_c2 = Buf("const2")
        S.op(DVE, lambda e: e.memset(epsc[:, :], EPS), writes=[b_c2], partial=True)
        S.op(DVE, lambda e: e.memset(lbc[:, :], 0.0), writes=[b_c2], partial=True)
        if NL == 2:
            tt(lbc[:, 8:16], vecs[:, VC + 48:VC + 56], vecs[:, 48:56], ALU.subtract, [b_const, b_c2], [b_c2])
            act_(lbc[:, 8:16], lbc[:, 8:16], AF.Sigmoid, [b_c2], [b_c2])
        ts(omlc[:, :], lbc[:, :], -1.0, 1.0, ALU.mult, ALU.add, [b_c2], [b_c2])
        for h in range(8):
            S.op(DVE, lambda e, h=h: e.memset(Sst[:, h, :], 0.0), writes=[b_S[h]])
        for g in range(3):
            for i in range(NBLK[g]):
                S.op(DVE, lambda e, g=g, i=i: e.memset(kc[g][:, i, :, :], 0.0), writes=[b_kc[g][i]])
                S.op(DVE, lambda e, g=g, i=i: e.memset(vc[g][:, i, :, :], 0.0), writes=[b_vc[g][i]])
        posi = Tt[5][:, :].bitcast(I32)
        ki = Tt[6][:, :].bitcast(I32)
        b_posi = b_T[5]
        TWO_PI = 2.0 * math.pi
        for ch in range(T // TT):
            cs = slice(ch * TT, (ch + 1) * TT)
            S.dma(SP, lambda e, cs=cs: e.dma_start(out=posi[:, :], in_=pos_in[:, cs]), writes=[b_posi], partial=False)
            A, Bq, C, Dd, E = Tt[0], Tt[1], Tt[2], Tt[3], Tt[4]
            bA, bB, bC, bD, bE = b_T[0], b_T[1], b_T[2], b_T[3], b_T[4]
            S.op(DVE, lambda e: e.tensor_copy(out=A[:, :], in_=posi[:, :]), reads=[b_posi], writes=[bA])
            fcol = cst["c_freq"]
            ts(A[:, :], A[:, :], fcol[:, 0:1], None, ALU.mult, None, [bA, b_const], [bA])
            ts(Bq[:, :], A[:, :], 1.0 / TWO_PI, None, ALU.mult, None, [bA], [bB])
            S.op(DVE, lambda e: e.tensor_copy(out=ki[:, :], in_=Bq[:, :]), reads=[bB], writes=[b_T[6]])
            S.op(DVE, lambda e: e.tensor_copy(out=Bq[:, :], in_=ki[:, :]), reads=[b_T[6]], writes=[bB])
            stt(C[:, :], Bq[:, :], -TWO_PI, A[:, :], ALU.mult, ALU.add, [bA, bB], [bC])

            def fold(R, bR, M, bM):
                ts(M[:, :], R[:, :], math.pi, None, ALU.is_gt, None, [bR], [bM])
                stt(R[:, :], M[:, :], -TWO_PI, R[:, :], ALU.mult, ALU.add, [bR, bM], [bR])
                ts(M[:, :], R[:, :], -math.pi, None, ALU.is_lt, None, [bR], [bM])
                stt(R[:, :], M[:, :], TWO_PI, R[:, :], ALU.mult, ALU.add, [bR, bM], [bR])
                ts(R[:, :], R[:, :], math.pi, -math.pi, ALU.min, ALU.max, [bR], [bR])
            fold(C, bC, Dd, bD)
            ts(E[:, :], C[:, :], math.pi / 2, None, ALU.add, None, [bC], [bE])
            fold(E, bE, Dd, bD)
            act_(C[:, :], C[:, :], AF.Sin, [bC], [bC])
            act_(E[:, :], E[:, :], AF.Sin, [bE], [bE])
            ts(C[:, :], C[:, :], fcol[:, 1:2], None, ALU.mult, None, [bC, b_const], [bC])
            S.dma(SP, lambda e, cs=cs: e.dma_start(out=ropec[:, cs], in_=E[:, :]), reads=[bE], writes=[b_rope], key=bE)
            S.dma(SP, lambda e, cs=cs: e.dma_start(out=ropes[:, cs], in_=C[:, :]), reads=[bC], writes=[b_rope], key=bC)

        def norm(src, b_src, nsub, gcol, dst, b_dst):
            for s in range(nsub):
                act_(junk[:, :], src[:, s, :], AF.Square, [b_src[s]], [b_junk, b_ss], accum=ss[:, s:s + 1], partial=True)
            act_(rs[:, 0:nsub], ss[:, 0:nsub], AF.Sqrt, [b_ss, b_c2], [b_rs], scale=1.0 / D, bias=epsc[:, 0:1])
            S.op(DVE, lambda e: e.reciprocal(out=rstd[:, 0:nsub], in_=rs[:, 0:nsub]), reads=[b_rs], writes=[b_rstd])
            for s in range(nsub):
                ts(xn[s][:, :], src[:, s, :], rstd[:, s:s + 1], None, ALU.mult, None, [b_src[s], b_rstd], [b_xn[s]])
                tb, btb = tbank()
                for c in range(8):
                    tr(tb[:, c * 128:(c + 1) * 128], xn[s][:, c * 128:(c + 1) * 128], [b_xn[s], b_const], [btb])
                gv = vecs[:, gcol:gcol + 8].unsqueeze(2).to_broadcast([128, 8, 128])
                tt(dst[:, :, s * 128:(s + 1) * 128], tb[:, :].rearrange("p (c t) -> p c t", t=128), gv, ALU.mult,
                   [btb, b_const], [b_dst], partial=True)

        def proj_fm(wt, bw, nk, jj, rhs_fn, rbufs, N):
            bk, bbk = bank()
            for k in range(nk):
                mm(bk[:, 0:N], wt[:, k, jj * 128:(jj + 1) * 128], rhs_fn(k), k == 0, k == nk - 1, [bw] + rbufs, [bbk])
            return bk, bbk

        def proj_tm_add(lhs, b_lhs, wap, l):
            for half in range(2):
                wt, bw = wtile(wap[l], 0, 8, half * 512)
                for s in range(NS):
                    bk, bbk = bank()
                    for k in range(8):
                        mm(bk[:, :], lhs[:, k, s * 128:(s + 1) * 128], wt[:, k, :], k == 0, k == 7, [b_lhs, bw], [bbk])
                    xs = x_sb[:, s, half * 512:(half + 1) * 512]
                    tt(xs, bk[:, :], xs, ALU.add, [bbk, b_x[s]], [b_x[s]])

        def ffn(l, wgu, wdn):
            for blk in range(6):
                wt, bw = wtile(wgu[l], 0, 8, blk * 512)
                for jj in range(4):
                    j = blk * 4 + jj
                    if j >= 22:
                        break
                    bk, bbk = proj_fm(wt, bw, 8, jj, lambda k: hT[:, k, :], [b_hT], TT)
                    act_(act[:, j, :], bk[:, 0:TT], AF.Silu, [bbk], [b_act], partial=True)
            for blk in range(5, 11):
                wt, bw = wtile(wgu[l], 0, 8, blk * 512)
                for jj in range(4):
                    col = blk * 512 + jj * 128
                    j = (col - DFF) // 128
                    if j < 0 or j >= 22:
                        continue
                    bk, bbk = proj_fm(wt, bw, 8, jj, lambda k: hT[:, k, :], [b_hT], TT)
                    tt(act[:, j, :], bk[:, 0:TT], act[:, j, :], ALU.mult, [bbk, b_act], [b_act], partial=True)
            for half in range(2):
                bks = [bank(hold=True) for _ in range(NS)]
                for r in range(3):
                    nk = 8 if r < 2 else 6
                    wt, bw = wtile(wdn[l], r * 1024, nk, half * 512)
                    for s in range(NS):
                        for jj in range(nk):
                            j = r * 8 + jj
                            mm(bks[s][0][:, :], act[:, j, s * 128:(s + 1) * 128], wt[:, jj, :], j == 0, j == 21,
                               [b_act, bw], [bks[s][1]])
                for s in range(NS):
                    xs = x_sb[:, s, half * 512:(half + 1) * 512]
                    stt(xs, bks[s][0][:, :], 0.5, xs, ALU.mult, ALU.add, [bks[s][1], b_x[s]], [b_x[s]])
                    release(bks[s][1])

        def rope_chunk(bk, bbk, scale, writer):
            i = rot("raw", 2)
            act_(raw[i][:, :], bk[:, 0:TT], AF.Copy, [bbk], [b_raw[i]], scale=scale)
            b2, bb2 = bank()
            pm = cst["c_perm"]
            mm(b2[:, 0:TT], pm[:, :], raw[i][:, :], True, True, [b_raw[i], b_const], [bb2])
            tt(rt1[i][:, :], raw[i][:, :], ropc[:, :], ALU.mult, [b_raw[i], b_ropt], [b_rt1[i]])
            tt(rt2[i][:, :], b2[:, 0:TT], rops[:, :], ALU.mult, [bb2, b_ropt], [b_rt2[i]])
            writer(rt1[i], rt2[i], [b_rt1[i], b_rt2[i]])

        def mixer(l, tau):
            win = W["w_in"][l]
            t0 = tau * TT
            S.dma(SP, lambda e: e.dma_start(out=ropc[:, :], in_=ropec[:, t0:t0 + TT]), reads=[b_rope], writes=[b_ropt], partial=False)
            S.dma(SP, lambda e: e.dma_start(out=rops[:, :], in_=ropes[:, t0:t0 + TT]), reads=[b_rope], writes=[b_ropt], partial=True)
            hfn = lambda k: hT[:, k, :]
            for g in range(3):
                wt, bw = wtile(win, 0, 8, (3 + g) * 512)
                for h in range(4):
                    bk, bbk = proj_fm(wt, bw, 8, h, hfn, [b_hT], TT)

                    def wr(t1, t2, rd, g=g, h=h):
                        for s in range(NS):
                            slot = (tau * NS + s) % NBLK[g]
                            tt(kc[g][:, slot, h, :], t1[:, s * 128:(s + 1) * 128], t2[:, s * 128:(s + 1) * 128], ALU.add,
                               rd, [b_kc[g][slot]], partial=True)
                    rope_chunk(bk, bbk, 1.0, wr)
            for g in range(3):
                wt, bw = wtile(win, 0, 8, g * 512)
                for h in range(4):
                    bk, bbk = proj_fm(wt, bw, 8, h, hfn, [b_hT], TT)

                    def wr(t1, t2, rd, g=g, h=h):
                        tt(qa[:, g * 4 + h, :], t1[:, :], t2[:, :], ALU.add, rd, [b_qa[g * 4 + h]])
                    rope_chunk(bk, bbk, 128.0 ** -0.5, wr)
            for g in range(3):
                wt, bw = wtile(win, 0, 8, (6 + g) * 512)
                for h in range(4):
                    bk, bbk = proj_fm(wt, bw, 8, h, hfn, [b_hT], TT)
                    i = rot("raw", 2)
                    act_(raw[i][:, :], bk[:, 0:TT], AF.Copy, [bbk], [b_raw[i]])
                    tb, btb = tbank()
                    for s in range(NS):
                        tr(tb[:, s * 128:(s + 1) * 128], raw[i][:, s * 128:(s + 1) * 128], [b_raw[i], b_const], [btb])
                    for s in range(NS):
                        slot = (tau * NS + s) % NBLK[g]
                        S.op(ACT, lambda e, g=g, slot=slot, h=h, s=s, tb=tb: e.copy(out=vc[g][:, slot, h, :], in_=tb[:, s * 128:(s + 1) * 128]),
                             reads=[btb], writes=[b_vc[g][slot]], partial=True)
            ones = cst["c_ones"]
            msk = cst["c_mask"]
            for h in range(4):
                for i in range(NS):
                    qb = tau * NS + i
                    ents = []
                    for g in range(3):
                        grp = [(g, dl) for dl in range(DMAX[g] + 1) if qb - dl >= 0]
                        for a in range(0, len(grp), 4):
                            ents.append(grp[a:a + 4])
                    nb_, bnb = bank(hold=True)
                    db_, bdb = bank(hold=True)
                    total = sum(len(c) for c in ents)
                    cnt = 0
                    for chunk in ents:
                        sbk, bsb = bank()
                        n = len(chunk)
                        g = chunk[0][0]
                        for idx, (g_, dl) in enumerate(chunk):
                            slot = (qb - dl) % NBLK[g]
                            mm(sbk[:, idx * 128:(idx + 1) * 128], kc[g][:, slot, h, :], qa[:, g * 4 + h, i * 128:(i + 1) * 128],
                               True, True, [b_kc[g][slot], b_qa[g * 4 + h]], [bsb])
                        pi = rot("pe", 2)
                        act_(pe_[pi][:, 0:n * 128], sbk[:, 0:n * 128], AF.Exp, [bsb], [b_pe[pi]])
                        mi = MBASE[g] + chunk[0][1]
                        tt(pm_[pi][:, 0:n * 128].rearrange("p (c t) -> p c t", t=128),
                           pe_[pi][:, 0:n * 128].rearrange("p (c t) -> p c t", t=128),
                           msk[:, mi:mi + n, :], ALU.mult, [b_pe[pi], b_const], [b_pm[pi]])
                        for idx, (g_, dl) in enumerate(chunk):
                            slot = (qb - dl) % NBLK[g]
                            mm(nb_[:, 0:128], vc[g][:, slot, h, :], pm_[pi][:, idx * 128:(idx + 1) * 128],
                               cnt == 0, cnt == total - 1, [b_vc[g][slot], b_pm[pi]], [bnb])
                            mm(db_[:, 0:128], ones[:, :], pm_[pi][:, idx * 128:(idx + 1) * 128],
                               cnt == 0, cnt == total - 1, [b_const, b_pm[pi]], [bdb])
                            cnt += 1
                    ri = rot("rec", 2)
                    S.op(DVE, lambda e, ri=ri, db_=db_: e.reciprocal(out=rec[ri][:, 0:128], in_=db_[:, 0:128]), reads=[bdb], writes=[b_rec[ri]])
                    tt(oa[:, h, i * 128:(i + 1) * 128], nb_[:, 0:128], rec[ri][:, 0:128], ALU.mult, [bnb, b_rec[ri]], [b_oa], partial=True)
                    release(bnb)
                    release(bdb)
            for blk in range(2):
                wt, bw = wtile(win, 0, 8, (13 + blk) * 512)
                for s in range(NS):
                    bk, bbk = bank()
                    for k in range(8):
                        mm(bk[:, :], hT[:, k, s * 128:(s + 1) * 128], wt[:, k, :], k == 0, k == 7, [b_hT, bw], [bbk])
                    S.op(ACT, lambda e, s=s, blk=blk, bk=bk: e.copy(out=iTM[:, s, blk * 512:(blk + 1) * 512], in_=bk[:, :]),
                         reads=[bbk], writes=[b_iTM], partial=True)
            NCH = TT // 64
            csc = cst["c_scan"]
            for blk in range(2):
                wf, bwf = wtile(win, 0, 8, (11 + blk) * 512)
                wq_, bwq = wtile(win, 0, 8, (9 + blk) * 512)
                for hh in range(4):
                    h = blk * 4 + hh
                    fp, bfp = proj_fm(wf, bwf, 8, hh, hfn, [b_hT], TT)
                    T1, T2, T3, T4, T5, T6, T7, T8 = [Tt[i] for i in range(8)]
                    B1, B2, B3, B4, B5, B6, B7, B8 = [b_T[i] for i in range(8)]
                    lcol = l * 8 + h
                    act_(T1[:, :], fp[:, 0:TT], AF.Sigmoid, [bfp], [B1])
                    ts(T2[:, :], T1[:, :], omlc[:, lcol:lcol + 1], lbc[:, lcol:lcol + 1], ALU.mult, ALU.add, [B1, b_c2], [B2])
                    act_(T3[:, :], T2[:, :], AF.Ln, [B2], [B3])
                    ts(T4[:, :], T2[:, :], -1.0, 1.0, ALU.mult, ALU.add, [B2], [B4])
                    S.op(DVE, lambda e, T5=T5, T3=T3: e.tensor_tensor_scan(out=T5[:, :], data0=csc[:, :], data1=T3[:, :], initial=0.0,
                                                                             op0=ALU.mult, op1=ALU.add), reads=[B3, b_const], writes=[B5])
                    v5 = T5[:, :].rearrange("p (c t) -> p c t", t=64)
                    bmid = T5[:, 31::64].unsqueeze(2).to_broadcast([128, NCH, 64])
                    tt(T6[:, :].rearrange("p (c t) -> p c t", t=64), v5, bmid, ALU.subtract, [B5], [B6])
                    act_(T7[:, :], T6[:, :], AF.Exp, [B6], [B7])
                    act_(T8[:, :], T6[:, :], AF.Exp, [B6], [B8], scale=-1.0)
                    act_(dec[:, h, :], T5[:, 63::64], AF.Exp, [B5], [b_dec[h]], partial=True)
                    act_(emid[:, h, :], T5[:, 31::64], AF.Exp, [B5], [b_dec[h]], partial=True)
                    act_(el[:, h, :], T6[:, 63::64], AF.Exp, [B6], [b_dec[h]], partial=True)
                    tt(kd[:, h, :], T4[:, :], T8[:, :], ALU.mult, [B4, B8], [b_kd[h]])
                    elb = el[:, h, :].unsqueeze(2).to_broadcast([128, NCH, 64])
                    tt(klb[:, :].rearrange("p (c t) -> p c t", t=64), kd[:, h, :].rearrange("p (c t) -> p c t", t=64), elb, ALU.mult,
                       [b_kd[h], b_dec[h]], [b_klb])
                    tb, btb = tbank()
                    for s in range(NS):
                        tr(tb[:, s * 128:(s + 1) * 128], klb[:, s * 128:(s + 1) * 128], [b_klb, b_const], [btb])
                    S.op(ACT, lambda e, h=h, tb=tb: e.copy(out=klT[:, h, :, :], in_=tb[:, 0:NS * 128].rearrange("p (s n) -> p s n", n=128)),
                         reads=[btb], writes=[b_klT[h]])
                    qp, bqp = proj_fm(wq_, bwq, 8, hh, hfn, [b_hT], TT)
                    stt(qd[:, h, :], qp[:, 0:TT], 128.0 ** -0.5, T7[:, :], ALU.mult, ALU.mult, [bqp, B7], [b_qd[h]])
            caus = cst["c_caus"]
            for s in range(NS):
                for h in range(8):
                    ab, bab = bank()
                    mm(ab[:, 0:128], kd[:, h, s * 128:(s + 1) * 128], qd[:, h, s * 128:(s + 1) * 128], True, True, [b_kd[h], b_qd[h]], [bab])
                    ai = rot("AT", 2)
                    tt(ATb[ai][:, :], ab[:, 0:128], caus[:, :], ALU.mult, [bab, b_const], [b_AT[ai]])
                    obk, bob = bank(hold=True)
                    mm(obk[:, 0:128], iTM[:, s, h * 128:(h + 1) * 128], ATb[ai][:, :], True, False, [b_iTM, b_AT[ai]], [bob])
                    for cc in range(2):
                        c = 2 * s + cc
                        ts(Stb[:, h, :], Sst[:, h, :], emid[:, h, c:c + 1], None, ALU.mult, None, [b_S[h], b_dec[h]], [b_St[h]])
                        mm(obk[:, cc * 64:(cc + 1) * 64], Stb[:, h, :], qd[:, h, c * 64:(c + 1) * 64], False, cc == 1, [b_St[h], b_qd[h]], [bob])
                        sbk, bsb = bank()
                        mm(sbk[:, 0:128], klT[cc * 64:(cc + 1) * 64, h, s, :], iTM[cc * 64:(cc + 1) * 64, s, h * 128:(h + 1) * 128], True, True,
                           [b_klT[h], b_iTM], [bsb])
                        stt(Sst[:, h, :], Sst[:, h, :], dec[:, h, c:c + 1], sbk[:, 0:128], ALU.mult, ALU.add, [b_S[h], b_dec[h], bsb], [b_S[h]])
                    S.op(ACT, lambda e, h=h, s=s, obk=obk: e.copy(out=oT[:, h, s * 128:(s + 1) * 128], in_=obk[:, 0:128]),
                         reads=[bob], writes=[b_oT[h]], partial=True)
                    release(bob)
            for blk in range(2):
                wo_, bwo = wtile(win, 0, 8, (15 + blk) * 512)
                for hh in range(4):
                    h = blk * 4 + hh
                    act_(sqb[:, :], oT[:, h, :], AF.Square, [b_oT[h]], [b_sqb])
                    bk, bbk = bank()
                    mm(bk[:, 0:TT], ones[:, :], sqb[:, :], True, True, [b_sqb, b_const], [bbk])
                    act_(Tt[0][:, :], bk[:, 0:TT], AF.Sqrt, [bbk, b_c2], [b_T[0]], scale=1.0 / 128, bias=epsc[:, 0:1])
                    S.op(DVE, lambda e: e.reciprocal(out=Tt[1][:, :], in_=Tt[0][:, :]), reads=[b_T[0]], writes=[b_T[1]])
                    tt(Tt[2][:, :], oT[:, h, :], Tt[1][:, :], ALU.mult, [b_oT[h], b_T[1]], [b_T[2]])
                    gp, bgp = proj_fm(wo_, bwo, 8, hh, hfn, [b_hT], TT)
                    act_(Tt[3][:, :], gp[:, 0:TT], AF.Silu, [bgp], [b_T[3]])
                    hn = l * VC + 40 + h
                    stt(ob[:, h, :], Tt[2][:, :], vecs[:, hn:hn + 1], Tt[3][:, :], ALU.mult, ALU.mult, [b_T[2], b_T[3], b_const], [b_ob], partial=True)
            for blk in range(2):
                wa, bwa = wtile(W["w_att_branch"][l], 0, 4, blk * 512)
                wb_, bwb = wtile(W["w_hgrn_branch"][l], 0, 8, blk * 512)
                wga, bwga = wtile(win, 0, 8, (17 + blk) * 512)
                wgb, bwgb = wtile(win, 0, 8, (19 + blk) * 512)
                for jj in range(4):
                    c = blk * 4 + jj
                    pa, bpa = proj_fm(wa, bwa, 4, jj, lambda k: oa[:, k, :], [b_oa], TT)
                    pga, bpga = proj_fm(wga, bwga, 8, jj, hfn, [b_hT], TT)
                    act_(Tt[4][:, :], pga[:, 0:TT], AF.Sigmoid, [bpga], [b_T[4]])
                    tt(ya[:, :], pa[:, 0:TT], Tt[4][:, :], ALU.mult, [bpa, b_T[4]], [b_ya])
                    pb_, bpb = proj_fm(wb_, bwb, 8, jj, lambda k: ob[:, k, :], [b_ob], TT)
                    pgb, bpgb = proj_fm(wgb, bwgb, 8, jj, hfn, [b_hT], TT)
                    act_(Tt[5][:, :], pgb[:, 0:TT], AF.Sigmoid, [bpgb], [b_T[5]])
                    tt(yb[:, :], pb_[:, 0:TT], Tt[5][:, :], ALU.mult, [bpb, b_T[5]], [b_yb])
                    tt(merged[:, c, :], ya[:, :], yb[:, :], ALU.add, [b_ya, b_yb], [b_merged], partial=True)
            proj_tm_add(merged, b_merged, W["w_mix_out"], l)

        def xattn(l, tau):
            if tau == 0:
                S.dma(SP, lambda e: e.dma_start(out=o_sb[:, :, :], in_=mem_in.rearrange("(s p) d -> p s d", p=128)), writes=b_o, partial=False)
                norm(o_sb, b_o, 2, l * VC + 24, qx, b_qx)
                for blk in range(2):
                    wt, bw = wtile(W["xattn_wkv"][l], 0, 8, blk * 512)
                    for jj in range(4):
                        bk, bbk = proj_fm(wt, bw, 8, jj, lambda k: qx[:, k, :], [b_qx], NMEM)
                        S.op(ACT, lambda e, blk=blk, jj=jj, bk=bk: e.copy(out=kxT[:, blk * 4 + jj, :], in_=bk[:, 0:NMEM]),
                             reads=[bbk], writes=[b_kxT], partial=True)
                for blk in range(2):
                    wt, bw = wtile(W["xattn_wkv"][l], 0, 8, 1024 + blk * 512)
                    for s in range(2):
                        bk, bbk = bank()
                        for k in range(8):
                            mm(bk[:, :], qx[:, k, s * 128:(s + 1) * 128], wt[:, k, :], k == 0, k == 7, [b_qx, bw], [bbk])
                        S.op(ACT, lambda e, s=s, blk=blk, bk=bk: e.copy(out=vx[:, s, blk * 512:(blk + 1) * 512], in_=bk[:, :]),
                             reads=[bbk], writes=[b_vx], partial=True)
            norm(x_sb, b_x, NS, l * VC + 16, hT, b_hT)
            for blk in range(2):
                wt, bw = wtile(W["xattn_wq"][l], 0, 8, blk * 512)
                for jj in range(4):
                    bk, bbk = proj_fm(wt, bw, 8, jj, lambda k: hT[:, k, :], [b_hT], TT)
                    S.op(ACT, lambda e, blk=blk, jj=jj, bk=bk: e.activation(out=qx[:, blk * 4 + jj, :], in_=bk[:, 0:TT], func=AF.Copy, scale=256.0 ** -0.5),
                         reads=[bbk], writes=[b_qx], partial=True)
            ones = cst["c_ones"]
            for h in range(4):
                for mc in range(2):
                    sbk, bsb = bank()
                    for dh in range(2):
                        mm(sbk[:, 0:TT], kxT[:, h * 2 + dh, mc * 128:(mc + 1) * 128], qx[:, h * 2 + dh, :], dh == 0, dh == 1, [b_kxT, b_qx], [bsb])
                    act_(pT[:, mc, :], sbk[:, 0:TT], AF.Exp, [bsb], [b_pT], partial=True)
                db_, bdb = bank()
                for mc in range(2):
                    mm(db_[:, 0:TT], ones[:, :], pT[:, mc, :], mc == 0, mc == 1, [b_const, b_pT], [bdb])
                ri = rot("rec", 2)
                S.op(DVE, lambda e, ri=ri, db_=db_: e.reciprocal(out=rec[ri][:, :], in_=db_[:, 0:TT]), reads=[bdb], writes=[b_rec[ri]])
                for dh in range(2):
                    obk, bob = bank()
                    for mc in range(2):
                        mm(obk[:, 0:TT], vx[:, mc, h * 256 + dh * 128:h * 256 + (dh + 1) * 128], pT[:, mc, :], mc == 0, mc == 1, [b_vx, b_pT], [bob])
                    tt(ox[:, h * 2 + dh, :], obk[:, 0:TT], rec[ri][:, :], ALU.mult, [bob, b_rec[ri]], [b_ox], partial=True)
            proj_tm_add(ox, b_ox, W["xattn_wo"], l)

        out_ops = []
        for l in range(NL):
            for tau in range(NT):
                t0 = tau * TT
                src = x_in if l == 0 else xscr
                rd = [] if l == 0 else [b_xscr]
                S.dma(SP, lambda e, src=src, t0=t0: e.dma_start(out=x_sb[:, :, :], in_=src[t0:t0 + TT, :].rearrange("(s p) d -> p s d", p=128)),
                      reads=rd, writes=b_x, partial=False)
                norm(x_sb, b_x, NS, l * VC + 0, hT, b_hT)
                ffn(l, W["ffn1_w_gu"], W["ffn1_w_down"])
                norm(x_sb, b_x, NS, l * VC + 8, hT, b_hT)
                mixer(l, tau)
                xattn(l, tau)
                norm(x_sb, b_x, NS, l * VC + 32, hT, b_hT)
                ffn(l, W["ffn2_w_gu"], W["ffn2_w_down"])
                if l == NL - 1:
                    for s in range(NS):
                        act_(junk[:, :], x_sb[:, s, :], AF.Square, [b_x[s]], [b_junk, b_ss], accum=ss[:, s:s + 1], partial=True)
                    act_(rs[:, 0:NS], ss[:, 0:NS], AF.Sqrt, [b_ss, b_c2], [b_rs], scale=1.0 / D, bias=epsc[:, 0:1])
                    S.op(DVE, lambda e: e.reciprocal(out=rstd[:, 0:NS], in_=rs[:, 0:NS]), reads=[b_rs], writes=[b_rstd])
                    for s in range(NS):
                        stt(o_sb[:, s, :], x_sb[:, s, :], rstd[:, s:s + 1], fing[:, :], ALU.mult, ALU.mult, [b_x[s], b_rstd, b_const], [b_o[s]])
                    o = S.dma(SP, lambda e, t0=t0: e.dma_start(out=out[t0:t0 + TT, :].rearrange("(s p) d -> p s d", p=128), in_=o_sb[:, :, :]),
                              reads=b_o, key=b_o[0])
                    out_ops.append(o)
                else:
                    o = S.dma(SP, lambda e, t0=t0: e.dma_start(out=xscr[t0:t0 + TT, :].rearrange("(s p) d -> p s d", p=128), in_=x_sb[:, :, :]),
                              reads=b_x, writes=[b_xscr], key=b_x[0])
            if l < NL - 1:
                for h in range(8):
                    S.op(DVE, lambda e, h=h: e.memset(Sst[:, h, :], 0.0), writes=[b_S[h]])

        S.finalize(nc, st)
        blk_ = st.enter_context(nc.Block())

        @blk_.tensor
        def _(e):
            S.emit_engine(PE, e)

        @blk_.scalar
        def _(e):
            S.emit_engine(ACT, e)

        @blk_.vector
        def _(e):
            S.emit_engine(DVE, e)

        @blk_.gpsimd
        def _(e):
            S.emit_engine(POOL, e)

        @blk_.sync
        def _(e):
            S.emit_engine(SP, e)
            for o in out_ops:
                e.wait_ge(o.ev[0], o.ev[1])
    return nc, S


def _layout_inputs(inp, b, T, NL):
    m = {}
    m["x"] = np.ascontiguousarray(inp["x"][b, :T])
    m["mem"] = np.ascontiguousarray(inp["mem"][b])
    m["posb"] = np.ascontiguousarray(np.broadcast_to(np.asarray(inp["positions"])[b, :T].astype(np.int32)[None, :], (128, T)))
    cols = []
    for l in range(NL):
        for n in ["ffn1_norm", "mix_norm", "xattn_norm", "mem_norm", "ffn2_norm", "hgrn_head_norm", "hgrn_lower_bounds"]:
            cols.append(np.asarray(inp[n])[l].reshape(8, 128).T)
    m["vecs"] = np.ascontiguousarray(np.concatenate(cols, axis=1).astype(np.float32))
    m["fing"] = np.ascontiguousarray(np.broadcast_to(np.asarray(inp["final_norm"]).astype(np.float32)[None, :], (128, D)))
    m.update(_host_consts())
    for n in ["ffn1_w_gu", "ffn1_w_down", "w_in", "w_att_branch", "w_hgrn_branch", "w_mix_out",
              "xattn_wq", "xattn_wkv", "xattn_wo", "ffn2_w_gu", "ffn2_w_down"]:
        m[n] = np.ascontiguousarray(np.asarray(inp[n])[:NL])
    return m


_CACHE = {}


def kernel(**inputs):
    inp = {k: np.asarray(v) for k, v in inputs.items()}
    B, T, _ = inp["x"].shape
    NL = inp["w_in"].shape[0]
    key = (T, NL)
    if key not in _CACHE:
        _CACHE[key] = build_program(T, NL)[0]
    nc = _CACHE[key]
    in_maps = [_layout_inputs(inp, b, T, NL) for b in range(B)]
    res = run_bass_kernel_spmd(nc, in_maps, core_ids=list(range(B)))
    return np.stack([np.asarray(r["out"]).reshape(T, D) for r in res.results], axis=0).astype(np.float32)
```

```python
import math
import numpy as np
import ml_dtypes
from contextlib import ExitStack
import concourse.bass as bass
import concourse.mybir as mybir
from concourse.bass_utils import run_bass_kernel_spmd

F32 = mybir.dt.float32
BF16 = mybir.dt.bfloat16
I32 = mybir.dt.int32
AF = mybir.ActivationFunctionType
ALU = mybir.AluOpType

PE, ACT, DVE, POOL, SP = "pe", "act", "dve", "pool", "sp"
SEM_CAP = 30000
D = 1024
DFF = 2816
NIN = 10752
TT = 256
NS = TT // 128
NMEM = 256
EPS = 1e-6
GROUPS = ((128, 1), (512, 4), (2048, 16))
DMAX = (1, 4, 16)
MBASE = (0, 2, 7)
NMASK = 24
NBLK = (DMAX[0] + NS, DMAX[1] + NS, DMAX[2] + NS)


class Buf:
    __slots__ = ("name", "writers", "readers", "prev_readers", "dma_sem", "dma_cnt")

    def __init__(self, name):
        self.name = name
        self.writers = []
        self.readers = []
        self.prev_readers = []
        self.dma_sem = None
        self.dma_cnt = 0


class Op:
    __slots__ = ("eng", "fn", "deps", "is_dma", "sig", "ev", "dsem")

    def __init__(self, eng, fn, is_dma):
        self.eng = eng
        self.fn = fn
        self.deps = []
        self.is_dma = is_dma
        self.sig = False
        self.ev = None
        self.dsem = None


class Sched:
    def __init__(self):
        self.ops = {PE: [], ACT: [], DVE: [], POOL: [], SP: []}
        self.all_ops = []
        self.dma_keys = []

    def _add(self, eng, fn, reads, writes, partial, is_dma, dma_key=None):
        op = Op(eng, fn, is_dma)
        deps = op.deps
        reads = list(dict.fromkeys(reads))
        writes = list(dict.fromkeys(writes))
        for b in reads:
            deps.extend(b.writers)
            b.readers.append(op)
        for b in writes:
            rd = [r for r in b.readers if r is not op]
            if rd:
                deps.extend(rd)
                deps.extend(b.writers)
                b.prev_readers = rd
                b.readers = [op] if len(rd) != len(b.readers) else []
                b.writers = [op]
            else:
                deps.extend(b.prev_readers)
                if not partial:
                    deps.extend(b.writers)
                    b.writers = [op]
                else:
                    b.writers.append(op)
        if is_dma:
            key = dma_key
            if key is None:
                key = writes[0] if writes else reads[0]
            op.dsem = key
            if key.dma_sem is None:
                key.dma_sem = True
                self.dma_keys.append(key)
        self.ops[eng].append(op)
        self.all_ops.append(op)
        return op

    def op(self, eng, fn, reads=(), writes=(), partial=False):
        return self._add(eng, fn, list(reads), list(writes), partial, False)

    def dma(self, eng, fn, reads=(), writes=(), partial=True, key=None):
        return self._add(eng, fn, list(reads), list(writes), partial, True, key)

    def finalize(self, nc, stack):
        for op in self.all_ops:
            for d in op.deps:
                if d.is_dma:
                    continue
                if d.eng == PE and op.eng == PE and not op.is_dma:
                    continue
                d.sig = True
        nsem = 0
        for eng in (PE, ACT, DVE, POOL):
            cnt = 0
            n = 0
            cur = None
            for op in self.ops[eng]:
                if op.is_dma or not op.sig:
                    continue
                if cur is None or cnt >= SEM_CAP:
                    cur = stack.enter_context(nc.semaphore(f"s_{eng}_{n}"))
                    n += 1
                    cnt = 0
                    nsem += 1
                cnt += 1
                op.ev = (cur, cnt)
        for key in self.dma_keys:
            key.dma_sem = stack.enter_context(nc.semaphore(f"d_{key.name}"))
            key.dma_cnt = 0
            nsem += 1
        for op in self.all_ops:
            if op.is_dma:
                k = op.dsem
                k.dma_cnt += 16
                op.ev = (k.dma_sem, k.dma_cnt)
        self.nsem = nsem

    def emit_engine(self, eng_name, eng):
        waited = {}
        for op in self.ops[eng_name]:
            need = {}
            for d in op.deps:
                if d.eng == PE and eng_name == PE and not d.is_dma and not op.is_dma:
                    continue
                sem, val = d.ev
                k = id(sem)
                if waited.get(k, 0) >= val:
                    continue
                if k not in need or need[k][1] < val:
                    need[k] = (sem, val)
            for k, (sem, val) in need.items():
                eng.wait_ge(sem, val)
                waited[k] = val
            ins = op.fn(eng)
            if op.is_dma:
                ins.then_inc(op.ev[0], 16)
            elif op.sig:
                ins.then_inc(op.ev[0], 1)


def _host_consts():
    c = {}
    c["c_ident"] = np.eye(128, dtype=np.float32).astype(ml_dtypes.bfloat16)
    perm = np.zeros((128, 128), np.float32)
    for dp in range(128):
        perm[(dp + 64) % 128, dp] = 1.0
    c["c_perm"] = perm.astype(ml_dtypes.bfloat16)
    c["c_ones"] = np.ones((128, 128), np.float32).astype(ml_dtypes.bfloat16)
    masks = np.zeros((128, NMASK, 128), np.float32)
    jj = np.arange(128)[:, None]
    ii = np.arange(128)[None, :]
    for g, (win, dil) in enumerate(GROUPS):
        for dl in range(DMAX[g] + 1):
            diff = 128 * dl + ii - jj
            ok = (diff >= 0) & (diff <= win) & (diff % dil == 0)
            masks[:, MBASE[g] + dl, :] = ok
    c["c_mask"] = masks.astype(ml_dtypes.bfloat16)
    caus = ((jj <= ii) & (jj // 64 == ii // 64)).astype(np.float32)
    c["c_caus"] = caus.astype(ml_dtypes.bfloat16)
    sc = np.ones((128, TT), np.float32)
    sc[:, ::64] = 0.0
    c["c_scan"] = sc
    fr = np.zeros((128, 2), np.float32)
    inv = 10000.0 ** (-np.arange(0, 128, 2, dtype=np.float32) / 128.0)
    fr[:, 0] = np.concatenate([inv, inv])
    fr[:64, 1] = -1.0
    fr[64:, 1] = 1.0
    c["c_freq"] = fr
    return c


CONST_SPECS = [("c_ident", [128, 128], BF16), ("c_perm", [128, 128], BF16), ("c_ones", [128, 128], BF16),
               ("c_mask", [128, NMASK, 128], BF16), ("c_caus", [128, 128], BF16),
               ("c_scan", [128, TT], F32), ("c_freq", [128, 2], F32)]

VC = 56


def build_program(T, NL):
    NT = T // TT
    nc = bass.Bass("TRN2", target_bir_lowering=False)
    S = Sched()
    dr = {}

    def din(name, shape, dt):
        dr[name] = nc.dram_tensor(name, shape, dt, kind="ExternalInput").ap()
        return dr[name]

    x_in = din("x", [T, D], F32)
    mem_in = din("mem", [NMEM, D], F32)
    pos_in = din("posb", [128, T], I32)
    vecs_in = din("vecs", [128, NL * VC], F32)
    fin_in = din("fing", [128, D], F32)
    for n, sh, dt in CONST_SPECS:
        din(n, sh, dt)
    W = {}
    for n, r, c in [("ffn1_w_gu", D, 2 * DFF), ("ffn1_w_down", DFF, D), ("w_in", D, NIN),
                    ("w_att_branch", 512, D), ("w_hgrn_branch", D, D), ("w_mix_out", D, D),
                    ("xattn_wq", D, D), ("xattn_wkv", D, 2 * D), ("xattn_wo", D, D),
                    ("ffn2_w_gu", D, 2 * DFF), ("ffn2_w_down", DFF, D)]:
        W[n] = din(n, [NL, r, c], F32)
    out = nc.dram_tensor("out", [T, D], F32, kind="ExternalOutput").ap()
    xscr = nc.dram_tensor("xscr", [T, D], F32).ap()
    NWT = NL * 80
    wscr = nc.dram_tensor("wscr", [NWT, 128, 8 * 512], BF16).ap()
    ropec = nc.dram_tensor("ropec", [128, T], F32).ap()
    ropes = nc.dram_tensor("ropes", [128, T], F32).ap()
    b_xscr = Buf("xscr")
    b_rope = Buf("ropescr")

    st = ExitStack()
    with st:
        def sb(name, shape, dt):
            return st.enter_context(nc.sbuf_tensor("sb_" + name, shape, dt))

        cst = {}
        b_const = Buf("const")
        for n, sh, dt in CONST_SPECS:
            cst[n] = sb("s" + n, sh, dt)
        vecs = sb("vecs", [128, NL * VC], F32)
        fing = sb("fing", [128, D], F32)
        epsc = sb("epsc", [128, 1], F32)
        lbc = sb("lbc", [128, NL * 8], F32)
        omlc = sb("omlc", [128, NL * 8], F32)
        x_sb = sb("x_sb", [128, NS, D], F32)
        b_x = [Buf(f"x{s}") for s in range(NS)]
        xn = [sb(f"xn{s}", [128, D], BF16) for s in range(NS)]
        b_xn = [Buf(f"xn{s}") for s in range(NS)]
        b_junk = Buf("junk")
        ss = sb("ss", [128, 4], F32)
        rs = sb("rs", [128, 4], F32)
        rstd = sb("rstd", [128, 4], F32)
        b_ss, b_rs, b_rstd = Buf("ss"), Buf("rs"), Buf("rstd")
        hT = sb("hT", [128, 8, TT], BF16)
        b_hT = Buf("hT")
        NSLOT = 4
        wsl = [sb(f"w{i}", [128, 8, 512], BF16) for i in range(NSLOT)]
        b_w = [Buf(f"w{i}") for i in range(NSLOT)]
        kc = [sb(f"kc{g}", [128, NBLK[g], 4, 128], BF16) for g in range(3)]
        vc = [sb(f"vc{g}", [128, NBLK[g], 4, 128], BF16) for g in range(3)]
        b_kc = [[Buf(f"kc{g}_{i}") for i in range(NBLK[g])] for g in range(3)]
        b_vc = [[Buf(f"vc{g}_{i}") for i in range(NBLK[g])] for g in range(3)]
        Sst = sb("Sst", [128, 8, 128], F32)
        Stb = sb("Stb", [128, 8, 128], BF16)
        b_S = [Buf(f"S{h}") for h in range(8)]
        b_St = [Buf(f"St{h}") for h in range(8)]
        kxT = sb("kxT", [128, 8, NMEM], BF16)
        vx = sb("vx", [128, 2, D], BF16)
        b_kxT, b_vx = Buf("kxT"), Buf("vx")
        ropc = sb("ropc", [128, TT], F32)
        rops = sb("rops", [128, TT], F32)
        b_ropt = Buf("ropt")
        act = sb("act", [128, 22, TT], BF16)
        b_act = Buf("act")
        qa = sb("qa", [128, 12, TT], BF16)
        junk = qa[:, 0:4, :].rearrange("p a b -> p (a b)")
        b_qa = [Buf(f"qa{i}") for i in range(12)]
        raw = [sb(f"raw{i}", [128, TT], BF16) for i in range(2)]
        b_raw = [Buf(f"raw{i}") for i in range(2)]
        rt1 = [sb(f"rt1_{i}", [128, TT], F32) for i in range(2)]
        rt2 = [sb(f"rt2_{i}", [128, TT], F32) for i in range(2)]
        b_rt1 = [Buf(f"rt1_{i}") for i in range(2)]
        b_rt2 = [Buf(f"rt2_{i}") for i in range(2)]
        pe_ = [sb(f"pe{i}", [128, 512], BF16) for i in range(2)]
        pm_ = [sb(f"pm{i}", [128, 512], BF16) for i in range(2)]
        b_pe = [Buf(f"pe{i}") for i in range(2)]
        b_pm = [Buf(f"pm{i}") for i in range(2)]
        rec = [sb(f"rec{i}", [128, TT], F32) for i in range(2)]
        b_rec = [Buf(f"rec{i}") for i in range(2)]
        oa = sb("oa", [128, 4, TT], BF16)
        b_oa = Buf("oa")
        Tt = [sb(f"T{i}", [128, TT], F32) for i in range(9)]
        b_T = [Buf(f"T{i}") for i in range(9)]
        klb = sb("klb", [128, TT], BF16)
        b_klb = Buf("klb")
        qd = sb("qd", [128, 8, TT], BF16)
        kd = sb("kd", [128, 8, TT], BF16)
        b_qd = [Buf(f"qd{h}") for h in range(8)]
        b_kd = [Buf(f"kd{h}") for h in range(8)]
        klT = sb("klT", [128, 8, NS, 128], BF16)
        b_klT = [Buf(f"klT{h}") for h in range(8)]
        dec = sb("dec", [128, 8, 4], F32)
        emid = sb("emid", [128, 8, 4], F32)
        el = sb("el", [128, 8, 4], F32)
        b_dec = [Buf(f"dec{h}") for h in range(8)]
        iTM = sb("iTM", [128, NS, D], BF16)
        b_iTM = Buf("iTM")
        ATb = [sb(f"AT{i}", [128, 128], BF16) for i in range(2)]
        b_AT = [Buf(f"AT{i}") for i in range(2)]
        oT = sb("oT", [128, 8, TT], F32)
        b_oT = [Buf(f"oT{h}") for h in range(8)]
        sqb = sb("sqb", [128, TT], BF16)
        b_sqb = Buf("sqb")
        ob = sb("ob", [128, 8, TT], BF16)
        b_ob = Buf("ob")
        ya = sb("ya", [128, TT], F32)
        yb = sb("yb", [128, TT], F32)
        b_ya, b_yb = Buf("ya"), Buf("yb")
        merged = sb("merged", [128, 8, TT], BF16)
        b_merged = Buf("merged")
        qx = qd
        b_qx = Buf("qx")
        pT = sb("pT", [128, 2, TT], BF16)
        b_pT = Buf("pT")
        ox = kd
        b_ox = Buf("ox")
        o_sb = act[:, 0:16, :].rearrange("p a b -> p (a b)").bitcast(F32).rearrange("p (s d) -> p s d", d=D)
        b_o = [b_act, b_act]
        NB = 6
        pbank = [st.enter_context(nc.psum_tensor(f"pb{i}", [128, 512], F32)) for i in range(NB)]
        b_pb = [Buf(f"pb{i}") for i in range(NB)]
        ptb = [st.enter_context(nc.psum_tensor(f"ptb{i}", [128, 1024], BF16)) for i in range(2)]
        b_ptb = [Buf(f"ptb{i}") for i in range(2)]
        ctr = {"bank": 0, "tb": 0, "w": 0, "raw": 0, "pe": 0, "rec": 0, "AT": 0}

        held = set()

        def bank(hold=False):
            while True:
                i = ctr["bank"] % NB
                ctr["bank"] += 1
                if i not in held:
                    break
            if hold:
                held.add(i)
            return pbank[i], b_pb[i]

        def release(bb):
            held.discard(b_pb.index(bb))

        def tbank():
            i = ctr["tb"] % 2
            ctr["tb"] += 1
            return ptb[i], b_ptb[i]

        def rot(name, n):
            i = ctr[name] % n
            ctr[name] += 1
            return i

        def mm(o, l, r, start, stop, reads, writes):
            S.op(PE, lambda e: e.matmul(o, lhsT=l, rhs=r, start=start, stop=stop), reads=reads, writes=writes, partial=True)

        def tr(o, i, reads, writes):
            idt = cst["c_ident"]
            S.op(PE, lambda e: e.transpose(o, i, idt[:, :]), reads=reads, writes=writes, partial=True)

        def act_(o, i, func, reads, writes, scale=1.0, bias=None, accum=None, partial=False):
            kw = {}
            if bias is not None:
                kw["bias"] = bias
            if accum is not None:
                kw["accum_out"] = accum
            S.op(ACT, lambda e: e.activation(out=o, in_=i, func=func, scale=scale, **kw), reads=reads, writes=writes, partial=partial)

        def tt(o, a, b, op, reads, writes, partial=False, eng=DVE):
            S.op(eng, lambda e: e.tensor_tensor(out=o, in0=a, in1=b, op=op), reads=reads, writes=writes, partial=partial)

        def ts(o, a, s1, s2, op0, op1, reads, writes, partial=False):
            if s2 is None:
                S.op(DVE, lambda e: e.tensor_scalar(out=o, in0=a, scalar1=s1, scalar2=None, op0=op0), reads=reads, writes=writes, partial=partial)
            else:
                S.op(DVE, lambda e: e.tensor_scalar(out=o, in0=a, scalar1=s1, scalar2=s2, op0=op0, op1=op1), reads=reads, writes=writes, partial=partial)

        def stt(o, a, sc, b, op0, op1, reads, writes, partial=False):
            S.op(DVE, lambda e: e.scalar_tensor_tensor(out=o, in0=a, scalar=sc, in1=b, op0=op0, op1=op1), reads=reads, writes=writes, partial=partial)

        wt_index = {}

        def wtile(name, l, r0, nk, c0):
            i = rot("w", NSLOT)
            dst = wsl[i][:, 0:nk, :]
            key = (name, l, r0, c0)
            if key not in wt_index:
                idx = len(wt_index)
                assert idx < NWT
                wt_index[key] = (idx, Buf(f"wscr{idx}"))
                bscr = wt_index[key][1]
                src = W[name][l][r0:r0 + nk * 128, c0:c0 + 512].rearrange("(k p) c -> p k c", p=128)
                S.dma(POOL, lambda e: e.dma_start(out=dst, in_=src), writes=[b_w[i]], partial=False)
                scr = wscr[idx].rearrange("p (k c) -> p k c", c=512)[:, 0:nk, :]
                S.dma(SP, lambda e: e.dma_start(out=scr, in_=dst), reads=[b_w[i]], writes=[bscr], key=b_w[i])
            else:
                idx, bscr = wt_index[key]
                scr = wscr[idx].rearrange("p (k c) -> p k c", c=512)[:, 0:nk, :]
                S.dma(SP, lambda e: e.dma_start(out=dst, in_=scr), reads=[bscr], writes=[b_w[i]], partial=False, key=b_w[i])
            return wsl[i], b_w[i]

        for n, sh, dt in CONST_SPECS:
            t_ = cst[n]
            full = t_[:, :, :] if len(sh) == 3 else t_[:, :]
            srcf = dr[n][:, :, :] if len(sh) == 3 else dr[n][:, :]
            S.dma(SP, lambda e, o=full, i=srcf: e.dma_start(out=o, in_=i), writes=[b_const])
        S.dma(SP, lambda e: e.dma_start(out=vecs[:, :], in_=vecs_in[:, :]), writes=[b_const])
        S.dma(SP, lambda e: e.dma_start(out=fing[:, :], in_=fin_in[:, :]), writes=[b_const])
        b_c2 = Buf("const2")
        S.op(DVE, lambda e: e.memset(epsc[:, :], EPS), writes=[b_c2], partial=True)
        S.op(DVE, lambda e: e.memset(lbc[:, :], 0.0), writes=[b_c2], partial=True)
        if NL == 2:
            tt(lbc[:, 8:16], vecs[:, VC + 48:VC + 56], vecs[:, 48:56], ALU.subtract, [b_const, b_c2], [b_c2])
            act_(lbc[:, 8:16], lbc[:, 8:16], AF.Sigmoid, [b_c2], [b_c2])
        ts(omlc[:, :], lbc[:, :], -1.0, 1.0, ALU.mult, ALU.add, [b_c2], [b_c2])
        for h in range(8):
            S.op(DVE, lambda e, h=h: e.memset(Sst[:, h, :], 0.0), writes=[b_S[h]])
        for g in range(3):
            for i in range(NBLK[g]):
                S.op(DVE, lambda e, g=g, i=i: e.memset(kc[g][:, i, :, :], 0.0), writes=[b_kc[g][i]])
                S.op(DVE, lambda e, g=g, i=i: e.memset(vc[g][:, i, :, :], 0.0), writes=[b_vc[g][i]])
        posi = Tt[5][:, :].bitcast(I32)
        ki = Tt[6][:, :].bitcast(I32)
        b_posi = b_T[5]
        TWO_PI = 2.0 * math.pi
        for ch in range(T // TT):
            cs = slice(ch * TT, (ch + 1) * TT)
            S.dma(SP, lambda e, cs=cs: e.dma_start(out=posi[:, :], in_=pos_in[:, cs]), writes=[b_posi], partial=False)
            A, Bq, C, Dd, E = Tt[0], Tt[1], Tt[2], Tt[3], Tt[4]
            bA, bB, bC, bD, bE = b_T[0], b_T[1], b_T[2], b_T[3], b_T[4]
            S.op(DVE, lambda e: e.tensor_copy(out=A[:, :], in_=posi[:, :]), reads=[b_posi], writes=[bA])
            fcol = cst["c_freq"]
            ts(A[:, :], A[:, :], fcol[:, 0:1], None, ALU.mult, None, [bA, b_const], [bA])
            ts(Bq[:, :], A[:, :], 1.0 / TWO_PI, None, ALU.mult, None, [bA], [bB])
            S.op(DVE, lambda e: e.tensor_copy(out=ki[:, :], in_=Bq[:, :]), reads=[bB], writes=[b_T[6]])
            S.op(DVE, lambda e: e.tensor_copy(out=Bq[:, :], in_=ki[:, :]), reads=[b_T[6]], writes=[bB])
            stt(C[:, :], Bq[:, :], -TWO_PI, A[:, :], ALU.mult, ALU.add, [bA, bB], [bC])

            def fold(R, bR, M, bM):
                ts(M[:, :], R[:, :], math.pi, None, ALU.is_gt, None, [bR], [bM])
                stt(R[:, :], M[:, :], -TWO_PI, R[:, :], ALU.mult, ALU.add, [bR, bM], [bR])
                ts(M[:, :], R[:, :], -math.pi, None, ALU.is_lt, None, [bR], [bM])
                stt(R[:, :], M[:, :], TWO_PI, R[:, :], ALU.mult, ALU.add, [bR, bM], [bR])
                ts(R[:, :], R[:, :], math.pi, -math.pi, ALU.min, ALU.max, [bR], [bR])
            fold(C, bC, Dd, bD)
            ts(E[:, :], C[:, :], math.pi / 2, None, ALU.add, None, [bC], [bE])
            fold(E, bE, Dd, bD)
            act_(C[:, :], C[:, :], AF.Sin, [bC], [bC])
            act_(E[:, :], E[:, :], AF.Sin, [bE], [bE])
            ts(C[:, :], C[:, :], fcol[:, 1:2], None, ALU.mult, None, [bC, b_const], [bC])
            S.dma(SP, lambda e, cs=cs: e.dma_start(out=ropec[:, cs], in_=E[:, :]), reads=[bE], writes=[b_rope], key=bE)
            S.dma(SP, lambda e, cs=cs: e.dma_start(out=ropes[:, cs], in_=C[:, :]), reads=[bC], writes=[b_rope], key=bC)

        def norm(src, b_src, nsub, gcol, dst, b_dst):
            for s in range(nsub):
                act_(junk[:, :], src[:, s, :], AF.Square, [b_src[s]], [b_junk, b_ss], accum=ss[:, s:s + 1], partial=True)
            act_(rs[:, 0:nsub], ss[:, 0:nsub], AF.Sqrt, [b_ss, b_c2], [b_rs], scale=1.0 / D, bias=epsc[:, 0:1])
            S.op(DVE, lambda e: e.reciprocal(out=rstd[:, 0:nsub], in_=rs[:, 0:nsub]), reads=[b_rs], writes=[b_rstd])
            for s in range(nsub):
                ts(xn[s][:, :], src[:, s, :], rstd[:, s:s + 1], None, ALU.mult, None, [b_src[s], b_rstd], [b_xn[s]])
                tb, btb = tbank()
                for c in range(8):
                    tr(tb[:, c * 128:(c + 1) * 128], xn[s][:, c * 128:(c + 1) * 128], [b_xn[s], b_const], [btb])
                gv = vecs[:, gcol:gcol + 8].unsqueeze(2).to_broadcast([128, 8, 128])
                tt(dst[:, :, s * 128:(s + 1) * 128], tb[:, :].rearrange("p (c t) -> p c t", t=128), gv, ALU.mult,
                   [btb, b_const], [b_dst], partial=True)

        def proj_fm(wt, bw, nk, jj, rhs_fn, rbufs, N):
            bk, bbk = bank()
            for k in range(nk):
                mm(bk[:, 0:N], wt[:, k, jj * 128:(jj + 1) * 128], rhs_fn(k), k == 0, k == nk - 1, [bw] + rbufs, [bbk])
            return bk, bbk

        def proj_tm_add(lhs, b_lhs, wap, l):
            for half in range(2):
                wt, bw = wtile(wap, l, 0, 8, half * 512)
                for s in range(NS):
                    bk, bbk = bank()
                    for k in range(8):
                        mm(bk[:, :], lhs[:, k, s * 128:(s + 1) * 128], wt[:, k, :], k == 0, k == 7, [b_lhs, bw], [bbk])
                    xs = x_sb[:, s, half * 512:(half + 1) * 512]
                    tt(xs, bk[:, :], xs, ALU.add, [bbk, b_x[s]], [b_x[s]])

        def ffn(l, wgu, wdn):
            for blk in range(6):
                wt, bw = wtile(wgu, l, 0, 8, blk * 512)
                for jj in range(4):
                    j = blk * 4 + jj
                    if j >= 22:
                        break
                    bk, bbk = proj_fm(wt, bw, 8, jj, lambda k: hT[:, k, :], [b_hT], TT)
                    act_(act[:, j, :], bk[:, 0:TT], AF.Silu, [bbk], [b_act], partial=True)
            for blk in range(5, 11):
                wt, bw = wtile(wgu, l, 0, 8, blk * 512)
                for jj in range(4):
                    col = blk * 512 + jj * 128
                    j = (col - DFF) // 128
                    if j < 0 or j >= 22:
                        continue
                    bk, bbk = proj_fm(wt, bw, 8, jj, lambda k: hT[:, k, :], [b_hT], TT)
                    tt(act[:, j, :], bk[:, 0:TT], act[:, j, :], ALU.mult, [bbk, b_act], [b_act], partial=True)
            for half in range(2):
                bks = [bank(hold=True) for _ in range(NS)]
                for r in range(3):
                    nk = 8 if r < 2 else 6
                    wt, bw = wtile(wdn, l, r * 1024, nk, half * 512)
                    for s in range(NS):
                        for jj in range(nk):
                            j = r * 8 + jj
                            mm(bks[s][0][:, :], act[:, j, s * 128:(s + 1) * 128], wt[:, jj, :], j == 0, j == 21,
                               [b_act, bw], [bks[s][1]])
                for s in range(NS):
                    xs = x_sb[:, s, half * 512:(half + 1) * 512]
                    stt(xs, bks[s][0][:, :], 0.5, xs, ALU.mult, ALU.add, [bks[s][1], b_x[s]], [b_x[s]])
                    release(bks[s][1])

        def rope_chunk(bk, bbk, scale, writer):
            i = rot("raw", 2)
            act_(raw[i][:, :], bk[:, 0:TT], AF.Copy, [bbk], [b_raw[i]], scale=scale)
            b2, bb2 = bank()
            pm = cst["c_perm"]
            mm(b2[:, 0:TT], pm[:, :], raw[i][:, :], True, True, [b_raw[i], b_const], [bb2])
            tt(rt1[i][:, :], raw[i][:, :], ropc[:, :], ALU.mult, [b_raw[i], b_ropt], [b_rt1[i]])
            tt(rt2[i][:, :], b2[:, 0:TT], rops[:, :], ALU.mult, [bb2, b_ropt], [b_rt2[i]])
            writer(rt1[i], rt2[i], [b_rt1[i], b_rt2[i]])

        def mixer(l, tau):
            t0 = tau * TT
            S.dma(SP, lambda e: e.dma_start(out=ropc[:, :], in_=ropec[:, t0:t0 + TT]), reads=[b_rope], writes=[b_ropt], partial=False)
            S.dma(SP, lambda e: e.dma_start(out=rops[:, :], in_=ropes[:, t0:t0 + TT]), reads=[b_rope], writes=[b_ropt], partial=True)
            hfn = lambda k: hT[:, k, :]
            for g in range(3):
                wt, bw = wtile("w_in", l, 0, 8, (3 + g) * 512)
                for h in range(4):
                    bk, bbk = proj_fm(wt, bw, 8, h, hfn, [b_hT], TT)

                    def wr(t1, t2, rd, g=g, h=h):
                        for s in range(NS):
                            slot = (tau * NS + s) % NBLK[g]
                            tt(kc[g][:, slot, h, :], t1[:, s * 128:(s + 1) * 128], t2[:, s * 128:(s + 1) * 128], ALU.add,
                               rd, [b_kc[g][slot]], partial=True)
                    rope_chunk(bk, bbk, 1.0, wr)
            for g in range(3):
                wt, bw = wtile("w_in", l, 0, 8, g * 512)
                for h in range(4):
                    bk, bbk = proj_fm(wt, bw, 8, h, hfn, [b_hT], TT)

                    def wr(t1, t2, rd, g=g, h=h):
                        tt(qa[:, g * 4 + h, :], t1[:, :], t2[:, :], ALU.add, rd, [b_qa[g * 4 + h]])
                    rope_chunk(bk, bbk, 128.0 ** -0.5, wr)
            for g in range(3):
                wt, bw = wtile("w_in", l, 0, 8, (6 + g) * 512)
                for h in range(4):
                    bk, bbk = proj_fm(wt, bw, 8, h, hfn, [b_hT], TT)
                    i = rot("raw", 2)
                    act_(raw[i][:, :], bk[:, 0:TT], AF.Copy, [bbk], [b_raw[i]])
                    tb, btb = tbank()
                    for s in range(NS):
                        tr(tb[:, s * 128:(s + 1) * 128], raw[i][:, s * 128:(s + 1) * 128], [b_raw[i], b_const], [btb])
                    for s in range(NS):
                        slot = (tau * NS + s) % NBLK[g]
                        S.op(ACT, lambda e, g=g, slot=slot, h=h, s=s, tb=tb: e.copy(out=vc[g][:, slot, h, :], in_=tb[:, s * 128:(s + 1) * 128]),
                             reads=[btb], writes=[b_vc[g][slot]], partial=True)
            ones = cst["c_ones"]
            msk = cst["c_mask"]
            for h in range(4):
                for i in range(NS):
                    qb = tau * NS + i
                    ents = []
                    for g in range(3):
                        grp = [(g, dl) for dl in range(DMAX[g] + 1) if qb - dl >= 0]
                        for a in range(0, len(grp), 4):
                            ents.append(grp[a:a + 4])
                    nb_, bnb = bank(hold=True)
                    db_, bdb = bank(hold=True)
                    total = sum(len(c) for c in ents)
                    cnt = 0
                    for chunk in ents:
                        sbk, bsb = bank()
                        n = len(chunk)
                        g = chunk[0][0]
                        for idx, (g_, dl) in enumerate(chunk):
                            slot = (qb - dl) % NBLK[g]
                            mm(sbk[:, idx * 128:(idx + 1) * 128], kc[g][:, slot, h, :], qa[:, g * 4 + h, i * 128:(i + 1) * 128],
                               True, True, [b_kc[g][slot], b_qa[g * 4 + h]], [bsb])
                        pi = rot("pe", 2)
                        act_(pe_[pi][:, 0:n * 128], sbk[:, 0:n * 128], AF.Exp, [bsb], [b_pe[pi]])
                        mi = MBASE[g] + chunk[0][1]
                        tt(pm_[pi][:, 0:n * 128].rearrange("p (c t) -> p c t", t=128),
                           pe_[pi][:, 0:n * 128].rearrange("p (c t) -> p c t", t=128),
                           msk[:, mi:mi + n, :], ALU.mult, [b_pe[pi], b_const], [b_pm[pi]])
                        for idx, (g_, dl) in enumerate(chunk):
                            slot = (qb - dl) % NBLK[g]
                            mm(nb_[:, 0:128], vc[g][:, slot, h, :], pm_[pi][:, idx * 128:(idx + 1) * 128],
                               cnt == 0, cnt == total - 1, [b_vc[g][slot], b_pm[pi]], [bnb])
                            mm(db_[:, 0:128], ones[:, :], pm_[pi][:, idx * 128:(idx + 1) * 128],
                               cnt == 0, cnt == total - 1, [b_const, b_pm[pi]], [bdb])
                            cnt += 1
                    ri = rot("rec", 2)
                    S.op(DVE, lambda e, ri=ri, db_=db_: e.reciprocal(out=rec[ri][:, 0:128], in_=db_[:, 0:128]), reads=[bdb], writes=[b_rec[ri]])
                    tt(oa[:, h, i * 128:(i + 1) * 128], nb_[:, 0:128], rec[ri][:, 0:128], ALU.mult, [bnb, b_rec[ri]], [b_oa], partial=True)
                    release(bnb)
                    release(bdb)
            for blk in range(2):
                wt, bw = wtile("w_in", l, 0, 8, (13 + blk) * 512)
                for s in range(NS):
                    bk, bbk = bank()
                    for k in range(8):
                        mm(bk[:, :], hT[:, k, s * 128:(s + 1) * 128], wt[:, k, :], k == 0, k == 7, [b_hT, bw], [bbk])
                    S.op(ACT, lambda e, s=s, blk=blk, bk=bk: e.copy(out=iTM[:, s, blk * 512:(blk + 1) * 512], in_=bk[:, :]),
                         reads=[bbk], writes=[b_iTM], partial=True)
            NCH = TT // 64
            csc = cst["c_scan"]
            for blk in range(2):
                wf, bwf = wtile("w_in", l, 0, 8, (11 + blk) * 512)
                wq_, bwq = wtile("w_in", l, 0, 8, (9 + blk) * 512)
                for hh in range(4):
                    h = blk * 4 + hh
                    fp, bfp = proj_fm(wf, bwf, 8, hh, hfn, [b_hT], TT)
                    T1, T2, T3, T4, T5, T6, T7, T8 = [Tt[i] for i in range(8)]
                    B1, B2, B3, B4, B5, B6, B7, B8 = [b_T[i] for i in range(8)]
                    lcol = l * 8 + h
                    act_(T1[:, :], fp[:, 0:TT], AF.Sigmoid, [bfp], [B1])
                    ts(T2[:, :], T1[:, :], omlc[:, lcol:lcol + 1], lbc[:, lcol:lcol + 1], ALU.mult, ALU.add, [B1, b_c2], [B2])
                    act_(T3[:, :], T2[:, :], AF.Ln, [B2], [B3])
                    ts(T4[:, :], T2[:, :], -1.0, 1.0, ALU.mult, ALU.add, [B2], [B4])
                    S.op(DVE, lambda e, T5=T5, T3=T3: e.tensor_tensor_scan(out=T5[:, :], data0=csc[:, :], data1=T3[:, :], initial=0.0,
                                                                             op0=ALU.mult, op1=ALU.add), reads=[B3, b_const], writes=[B5])
                    v5 = T5[:, :].rearrange("p (c t) -> p c t", t=64)
                    bmid = T5[:, 31::64].unsqueeze(2).to_broadcast([128, NCH, 64])
                    tt(T6[:, :].rearrange("p (c t) -> p c t", t=64), v5, bmid, ALU.subtract, [B5], [B6])
                    act_(T7[:, :], T6[:, :], AF.Exp, [B6], [B7])
                    act_(T8[:, :], T6[:, :], AF.Exp, [B6], [B8], scale=-1.0)
                    act_(dec[:, h, :], T5[:, 63::64], AF.Exp, [B5], [b_dec[h]], partial=True)
                    act_(emid[:, h, :], T5[:, 31::64], AF.Exp, [B5], [b_dec[h]], partial=True)
                    act_(el[:, h, :], T6[:, 63::64], AF.Exp, [B6], [b_dec[h]], partial=True)
                    tt(kd[:, h, :], T4[:, :], T8[:, :], ALU.mult, [B4, B8], [b_kd[h]])
                    elb = el[:, h, :].unsqueeze(2).to_broadcast([128, NCH, 64])
                    tt(klb[:, :].rearrange("p (c t) -> p c t", t=64), kd[:, h, :].rearrange("p (c t) -> p c t", t=64), elb, ALU.mult,
                       [b_kd[h], b_dec[h]], [b_klb])
                    tb, btb = tbank()
                    for s in range(NS):
                        tr(tb[:, s * 128:(s + 1) * 128], klb[:, s * 128:(s + 1) * 128], [b_klb, b_const], [btb])
                    S.op(ACT, lambda e, h=h, tb=tb: e.copy(out=klT[:, h, :, :], in_=tb[:, 0:NS * 128].rearrange("p (s n) -> p s n", n=128)),
                         reads=[btb], writes=[b_klT[h]])
                    qp, bqp = proj_fm(wq_, bwq, 8, hh, hfn, [b_hT], TT)
                    stt(qd[:, h, :], qp[:, 0:TT], 128.0 ** -0.5, T7[:, :], ALU.mult, ALU.mult, [bqp, B7], [b_qd[h]])
            caus = cst["c_caus"]
            for s in range(NS):
                for h in range(8):
                    ab, bab = bank()
                    mm(ab[:, 0:128], kd[:, h, s * 128:(s + 1) * 128], qd[:, h, s * 128:(s + 1) * 128], True, True, [b_kd[h], b_qd[h]], [bab])
                    ai = rot("AT", 2)
                    tt(ATb[ai][:, :], ab[:, 0:128], caus[:, :], ALU.mult, [bab, b_const], [b_AT[ai]])
                    obk, bob = bank(hold=True)
                    mm(obk[:, 0:128], iTM[:, s, h * 128:(h + 1) * 128], ATb[ai][:, :], True, False, [b_iTM, b_AT[ai]], [bob])
                    for cc in range(2):
                        c = 2 * s + cc
                        ts(Stb[:, h, :], Sst[:, h, :], emid[:, h, c:c + 1], None, ALU.mult, None, [b_S[h], b_dec[h]], [b_St[h]])
                        mm(obk[:, cc * 64:(cc + 1) * 64], Stb[:, h, :], qd[:, h, c * 64:(c + 1) * 64], False, cc == 1, [b_St[h], b_qd[h]], [bob])
                        sbk, bsb = bank()
                        mm(sbk[:, 0:128], klT[cc * 64:(cc + 1) * 64, h, s, :], iTM[cc * 64:(cc + 1) * 64, s, h * 128:(h + 1) * 128], True, True,
                           [b_klT[h], b_iTM], [bsb])
                        stt(Sst[:, h, :], Sst[:, h, :], dec[:, h, c:c + 1], sbk[:, 0:128], ALU.mult, ALU.add, [b_S[h], b_dec[h], bsb], [b_S[h]])
                    S.op(ACT, lambda e, h=h, s=s, obk=obk: e.copy(out=oT[:, h, s * 128:(s + 1) * 128], in_=obk[:, 0:128]),
                         reads=[bob], writes=[b_oT[h]], partial=True)
                    release(bob)
            for blk in range(2):
                wo_, bwo = wtile("w_in", l, 0, 8, (15 + blk) * 512)
                for hh in range(4):
                    h = blk * 4 + hh
                    act_(sqb[:, :], oT[:, h, :], AF.Square, [b_oT[h]], [b_sqb])
                    bk, bbk = bank()
                    mm(bk[:, 0:TT], ones[:, :], sqb[:, :], True, True, [b_sqb, b_const], [bbk])
                    act_(Tt[0][:, :], bk[:, 0:TT], AF.Sqrt, [bbk, b_c2], [b_T[0]], scale=1.0 / 128, bias=epsc[:, 0:1])
                    S.op(DVE, lambda e: e.reciprocal(out=Tt[1][:, :], in_=Tt[0][:, :]), reads=[b_T[0]], writes=[b_T[1]])
                    tt(Tt[2][:, :], oT[:, h, :], Tt[1][:, :], ALU.mult, [b_oT[h], b_T[1]], [b_T[2]])
                    gp, bgp = proj_fm(wo_, bwo, 8, hh, hfn, [b_hT], TT)
                    act_(Tt[3][:, :], gp[:, 0:TT], AF.Silu, [bgp], [b_T[3]])
                    hn = l * VC + 40 + h
                    stt(ob[:, h, :], Tt[2][:, :], vecs[:, hn:hn + 1], Tt[3][:, :], ALU.mult, ALU.mult, [b_T[2], b_T[3], b_const], [b_ob], partial=True)
            for blk in range(2):
                wa, bwa = wtile("w_att_branch", l, 0, 4, blk * 512)
                wb_, bwb = wtile("w_hgrn_branch", l, 0, 8, blk * 512)
                wga, bwga = wtile("w_in", l, 0, 8, (17 + blk) * 512)
                wgb, bwgb = wtile("w_in", l, 0, 8, (19 + blk) * 512)
                for jj in range(4):
                    c = blk * 4 + jj
                    pa, bpa = proj_fm(wa, bwa, 4, jj, lambda k: oa[:, k, :], [b_oa], TT)
                    pga, bpga = proj_fm(wga, bwga, 8, jj, hfn, [b_hT], TT)
                    act_(Tt[4][:, :], pga[:, 0:TT], AF.Sigmoid, [bpga], [b_T[4]])
                    tt(ya[:, :], pa[:, 0:TT], Tt[4][:, :], ALU.mult, [bpa, b_T[4]], [b_ya])
                    pb_, bpb = proj_fm(wb_, bwb, 8, jj, lambda k: ob[:, k, :], [b_ob], TT)
                    pgb, bpgb = proj_fm(wgb, bwgb, 8, jj, hfn, [b_hT], TT)
                    act_(Tt[5][:, :], pgb[:, 0:TT], AF.Sigmoid, [bpgb], [b_T[5]])
                    tt(yb[:, :], pb_[:, 0:TT], Tt[5][:, :], ALU.mult, [bpb, b_T[5]], [b_yb])
                    tt(merged[:, c, :], ya[:, :], yb[:, :], ALU.add, [b_ya, b_yb], [b_merged], partial=True)
            proj_tm_add(merged, b_merged, "w_mix_out", l)

        def xattn(l, tau):
            if tau == 0:
                S.dma(SP, lambda e: e.dma_start(out=o_sb[:, :, :], in_=mem_in.rearrange("(s p) d -> p s d", p=128)), writes=b_o, partial=False)
                norm(o_sb, b_o, 2, l * VC + 24, qx, b_qx)
                for blk in range(2):
                    wt, bw = wtile("xattn_wkv", l, 0, 8, blk * 512)
                    for jj in range(4):
                        bk, bbk = proj_fm(wt, bw, 8, jj, lambda k: qx[:, k, :], [b_qx], NMEM)
                        S.op(ACT, lambda e, blk=blk, jj=jj, bk=bk: e.copy(out=kxT[:, blk * 4 + jj, :], in_=bk[:, 0:NMEM]),
                             reads=[bbk], writes=[b_kxT], partial=True)
                for blk in range(2):
                    wt, bw = wtile("xattn_wkv", l, 0, 8, 1024 + blk * 512)
                    for s in range(2):
                        bk, bbk = bank()
                        for k in range(8):
                            mm(bk[:, :], qx[:, k, s * 128:(s + 1) * 128], wt[:, k, :], k == 0, k == 7, [b_qx, bw], [bbk])
                        S.op(ACT, lambda e, s=s, blk=blk, bk=bk: e.copy(out=vx[:, s, blk * 512:(blk + 1) * 512], in_=bk[:, :]),
                             reads=[bbk], writes=[b_vx], partial=True)
            norm(x_sb, b_x, NS, l * VC + 16, hT, b_hT)
            for blk in range(2):
                wt, bw = wtile("xattn_wq", l, 0, 8, blk * 512)
                for jj in range(4):
                    bk, bbk = proj_fm(wt, bw, 8, jj, lambda k: hT[:, k, :], [b_hT], TT)
                    S.op(ACT, lambda e, blk=blk, jj=jj, bk=bk: e.activation(out=qx[:, blk * 4 + jj, :], in_=bk[:, 0:TT], func=AF.Copy, scale=256.0 ** -0.5),
                         reads=[bbk], writes=[b_qx], partial=True)
            ones = cst["c_ones"]
            for h in range(4):
                for mc in range(2):
                    sbk, bsb = bank()
                    for dh in range(2):
                        mm(sbk[:, 0:TT], kxT[:, h * 2 + dh, mc * 128:(mc + 1) * 128], qx[:, h * 2 + dh, :], dh == 0, dh == 1, [b_kxT, b_qx], [bsb])
                    act_(pT[:, mc, :], sbk[:, 0:TT], AF.Exp, [bsb], [b_pT], partial=True)
                db_, bdb = bank()
                for mc in range(2):
                    mm(db_[:, 0:TT], ones[:, :], pT[:, mc, :], mc == 0, mc == 1, [b_const, b_pT], [bdb])
                ri = rot("rec", 2)
                S.op(DVE, lambda e, ri=ri, db_=db_: e.reciprocal(out=rec[ri][:, :], in_=db_[:, 0:TT]), reads=[bdb], writes=[b_rec[ri]])
                for dh in range(2):
                    obk, bob = bank()
                    for mc in range(2):
                        mm(obk[:, 0:TT], vx[:, mc, h * 256 + dh * 128:h * 256 + (dh + 1) * 128], pT[:, mc, :], mc == 0, mc == 1, [b_vx, b_pT], [bob])
                    tt(ox[:, h * 2 + dh, :], obk[:, 0:TT], rec[ri][:, :], ALU.mult, [bob, b_rec[ri]], [b_ox], partial=True)
            proj_tm_add(ox, b_ox, "xattn_wo", l)

        out_ops = []
        for l in range(NL):
            for tau in range(NT):
                t0 = tau * TT
                src = x_in if l == 0 else xscr
                rd = [] if l == 0 else [b_xscr]
                S.dma(SP, lambda e, src=src, t0=t0: e.dma_start(out=x_sb[:, :, :], in_=src[t0:t0 + TT, :].rearrange("(s p) d -> p s d", p=128)),
                      reads=rd, writes=b_x, partial=False)
                norm(x_sb, b_x, NS, l * VC + 0, hT, b_hT)
                ffn(l, "ffn1_w_gu", "ffn1_w_down")
                norm(x_sb, b_x, NS, l * VC + 8, hT, b_hT)
                mixer(l, tau)
                xattn(l, tau)
                norm(x_sb, b_x, NS, l * VC + 32, hT, b_hT)
                ffn(l, "ffn2_w_gu", "ffn2_w_down")
                if l == NL - 1:
                    for s in range(NS):
                        act_(junk[:, :], x_sb[:, s, :], AF.Square, [b_x[s]], [b_junk, b_ss], accum=ss[:, s:s + 1], partial=True)
                    act_(rs[:, 0:NS], ss[:, 0:NS], AF.Sqrt, [b_ss, b_c2], [b_rs], scale=1.0 / D, bias=epsc[:, 0:1])
                    S.op(DVE, lambda e: e.reciprocal(out=rstd[:, 0:NS], in_=rs[:, 0:NS]), reads=[b_rs], writes=[b_rstd])
                    for s in range(NS):
                        stt(o_sb[:, s, :], x_sb[:, s, :], rstd[:, s:s + 1], fing[:, :], ALU.mult, ALU.mult, [b_x[s], b_rstd, b_const], [b_o[s]])
                    o = S.dma(SP, lambda e, t0=t0: e.dma_start(out=out[t0:t0 + TT, :].rearrange("(s p) d -> p s d", p=128), in_=o_sb[:, :, :]),
                              reads=b_o, key=b_o[0])
                    out_ops.append(o)
                else:
                    o = S.dma(SP, lambda e, t0=t0: e.dma_start(out=xscr[t0:t0 + TT, :].rearrange("(s p) d -> p s d", p=128), in_=x_sb[:, :, :]),
                              reads=b_x, writes=[b_xscr], key=b_x[0])
            if l < NL - 1:
                for h in range(8):
                    S.op(DVE, lambda e, h=h: e.memset(Sst[:, h, :], 0.0), writes=[b_S[h]])

        S.finalize(nc, st)
        blk_ = st.enter_context(nc.Block())

        @blk_.tensor
        def _(e):
            S.emit_engine(PE, e)

        @blk_.scalar
        def _(e):
            S.emit_engine(ACT, e)

        @blk_.vector
        def _(e):
            S.emit_engine(DVE, e)

        @blk_.gpsimd
        def _(e):
            S.emit_engine(POOL, e)

        @blk_.sync
        def _(e):
            S.emit_engine(SP, e)
            for o in out_ops:
                e.wait_ge(o.ev[0], o.ev[1])
    return nc, S


def _layout_inputs(inp, b, T, NL):
    m = {}
    m["x"] = np.ascontiguousarray(inp["x"][b, :T])
    m["mem"] = np.ascontiguousarray(inp["mem"][b])
    m["posb"] = np.ascontiguousarray(np.broadcast_to(np.asarray(inp["positions"])[b, :T].astype(np.int32)[None, :], (128, T)))
    cols = []
    for l in range(NL):
        for n in ["ffn1_norm", "mix_norm", "xattn_norm", "mem_norm", "ffn2_norm", "hgrn_head_norm", "hgrn_lower_bounds"]:
            cols.append(np.asarray(inp[n])[l].reshape(8, 128).T)
    m["vecs"] = np.ascontiguousarray(np.concatenate(cols, axis=1).astype(np.float32))
    m["fing"] = np.ascontiguousarray(np.broadcast_to(np.asarray(inp["final_norm"]).astype(np.float32)[None, :], (128, D)))
    m.update(_host_consts())
    for n in ["ffn1_w_gu", "ffn1_w_down", "w_in", "w_att_branch", "w_hgrn_branch", "w_mix_out",
              "xattn_wq", "xattn_wkv", "xattn_wo", "ffn2_w_gu", "ffn2_w_down"]:
        m[n] = np.ascontiguousarray(np.asarray(inp[n])[:NL])
    return m


_CACHE = {}


def kernel(**inputs):
    inp = {k: np.asarray(v) for k, v in inputs.items()}
    B, T, _ = inp["x"].shape
    NL = inp["w_in"].shape[0]
    key = (T, NL)
    if key not in _CACHE:
        _CACHE[key] = build_program(T, NL)[0]
    nc = _CACHE[key]
    in_maps = [_layout_inputs(inp, b, T, NL) for b in range(B)]
    res = run_bass_kernel_spmd(nc, in_maps, core_ids=list(range(B)))
    return np.stack([np.asarray(r["out"]).reshape(T, D) for r in res.results], axis=0).astype(np.float32)
```

```python
import math
import numpy as np
import ml_dtypes
from contextlib import ExitStack
import concourse.bass as bass
import concourse.mybir as mybir
from concourse.bass_utils import run_bass_kernel_spmd

F32 = mybir.dt.float32
BF16 = mybir.dt.bfloat16
I32 = mybir.dt.int32
AF = mybir.ActivationFunctionType
ALU = mybir.AluOpType

PE, ACT, DVE, POOL, SP = "pe", "act", "dve", "pool", "sp"
SEM_CAP = 30000
D = 1024
DFF = 2816
NIN = 10752
TT = 256
NS = TT // 128
NMEM = 256
EPS = 1e-6
GROUPS = ((128, 1), (512, 4), (2048, 16))
DMAX = (1, 4, 16)
MBASE = (0, 2, 7)
NMASK = 24
NBLK = (DMAX[0] + NS, DMAX[1] + NS, DMAX[2] + NS)


class Buf:
    __slots__ = ("name", "writers", "readers", "prev_readers", "dma_sem", "dma_cnt")

    def __init__(self, name):
        self.name = name
        self.writers = []
        self.readers = []
        self.prev_readers = []
        self.dma_sem = None
        self.dma_cnt = 0


class Op:
    __slots__ = ("eng", "fn", "deps", "is_dma", "sig", "ev", "dsem")

    def __init__(self, eng, fn, is_dma):
        self.eng = eng
        self.fn = fn
        self.deps = []
        self.is_dma = is_dma
        self.sig = False
        self.ev = None
        self.dsem = None


class Sched:
    def __init__(self):
        self.ops = {PE: [], ACT: [], DVE: [], POOL: [], SP: []}
        self.all_ops = []
        self.dma_keys = []

    def _add(self, eng, fn, reads, writes, partial, is_dma, dma_key=None):
        op = Op(eng, fn, is_dma)
        deps = op.deps
        reads = list(dict.fromkeys(reads))
        writes = list(dict.fromkeys(writes))
        for b in reads:
            deps.extend(b.writers)
            b.readers.append(op)
        for b in writes:
            rd = [r for r in b.readers if r is not op]
            if rd:
                deps.extend(rd)
                deps.extend(b.writers)
                b.prev_readers = rd
                b.readers = [op] if len(rd) != len(b.readers) else []
                b.writers = [op]
            else:
                deps.extend(b.prev_readers)
                if not partial:
                    deps.extend(b.writers)
                    b.writers = [op]
                else:
                    b.writers.append(op)
        if is_dma:
            key = dma_key
            if key is None:
                key = writes[0] if writes else reads[0]
            op.dsem = key
            if key.dma_sem is None:
                key.dma_sem = True
                self.dma_keys.append(key)
        self.ops[eng].append(op)
        self.all_ops.append(op)
        return op

    def op(self, eng, fn, reads=(), writes=(), partial=False):
        return self._add(eng, fn, list(reads), list(writes), partial, False)

    def dma(self, eng, fn, reads=(), writes=(), partial=True, key=None):
        return self._add(eng, fn, list(reads), list(writes), partial, True, key)

    def finalize(self, nc, stack):
        for op in self.all_ops:
            for d in op.deps:
                if d.is_dma:
                    continue
                if d.eng == PE and op.eng == PE and not op.is_dma:
                    continue
                d.sig = True
        nsem = 0
        for eng in (PE, ACT, DVE, POOL):
            cnt = 0
            n = 0
            cur = None
            for op in self.ops[eng]:
                if op.is_dma or not op.sig:
                    continue
                if cur is None or cnt >= SEM_CAP:
                    cur = stack.enter_context(nc.semaphore(f"s_{eng}_{n}"))
                    n += 1
                    cnt = 0
                    nsem += 1
                cnt += 1
                op.ev = (cur, cnt)
        for key in self.dma_keys:
            key.dma_sem = stack.enter_context(nc.semaphore(f"d_{key.name}"))
            key.dma_cnt = 0
            nsem += 1
        for op in self.all_ops:
            if op.is_dma:
                k = op.dsem
                k.dma_cnt += 16
                op.ev = (k.dma_sem, k.dma_cnt)
        self.nsem = nsem

    def emit_engine(self, eng_name, eng):
        waited = {}
        for op in self.ops[eng_name]:
            need = {}
            for d in op.deps:
                if d.eng == PE and eng_name == PE and not d.is_dma and not op.is_dma:
                    continue
                sem, val = d.ev
                k = id(sem)
                if waited.get(k, 0) >= val:
                    continue
                if k not in need or need[k][1] < val:
                    need[k] = (sem, val)
            for k, (sem, val) in need.items():
                eng.wait_ge(sem, val)
                waited[k] = val
            ins = op.fn(eng)
            if op.is_dma:
                ins.then_inc(op.ev[0], 16)
            elif op.sig:
                ins.then_inc(op.ev[0], 1)


def _host_consts():
    c = {}
    c["c_ident"] = np.eye(128, dtype=np.float32).astype(ml_dtypes.bfloat16)
    perm = np.zeros((128, 128), np.float32)
    for dp in range(128):
        perm[(dp + 64) % 128, dp] = 1.0
    c["c_perm"] = perm.astype(ml_dtypes.bfloat16)
    c["c_ones"] = np.ones((128, 128), np.float32).astype(ml_dtypes.bfloat16)
    masks = np.zeros((128, NMASK, 128), np.float32)
    jj = np.arange(128)[:, None]
    ii = np.arange(128)[None, :]
    for g, (win, dil) in enumerate(GROUPS):
        for dl in range(DMAX[g] + 1):
            diff = 128 * dl + ii - jj
            ok = (diff >= 0) & (diff <= win) & (diff % dil == 0)
            masks[:, MBASE[g] + dl, :] = ok
    c["c_mask"] = masks.astype(ml_dtypes.bfloat16)
    caus = ((jj <= ii) & (jj // 64 == ii // 64)).astype(np.float32)
    c["c_caus"] = caus.astype(ml_dtypes.bfloat16)
    sc = np.ones((128, TT), np.float32)
    sc[:, ::64] = 0.0
    c["c_scan"] = sc
    fr = np.zeros((128, 2), np.float32)
    inv = 10000.0 ** (-np.arange(0, 128, 2, dtype=np.float32) / 128.0)
    fr[:, 0] = np.concatenate([inv, inv])
    fr[:64, 1] = -1.0
    fr[64:, 1] = 1.0
    c["c_freq"] = fr
    return c


CONST_SPECS = [("c_ident", [128, 128], BF16), ("c_perm", [128, 128], BF16), ("c_ones", [128, 128], BF16),
               ("c_mask", [128, NMASK, 128], BF16), ("c_caus", [128, 128], BF16),
               ("c_scan", [128, TT], F32), ("c_freq", [128, 2], F32)]

VC = 56


def build_program(T, NL):
    NT = T // TT
    nc = bass.Bass("TRN2", target_bir_lowering=False)
    S = Sched()
    dr = {}

    def din(name, shape, dt):
        dr[name] = nc.dram_tensor(name, shape, dt, kind="ExternalInput").ap()
        return dr[name]

    x_in = din("x", [T, D], F32)
    mem_in = din("mem", [NMEM, D], F32)
    pos_in = din("posb", [128, T], I32)
    vecs_in = din("vecs", [128, NL * VC], F32)
    fin_in = din("fing", [128, D], F32)
    for n, sh, dt in CONST_SPECS:
        din(n, sh, dt)
    W = {}
    for n, r, c in [("ffn1_w_gu", D, 2 * DFF), ("ffn1_w_down", DFF, D), ("w_in", D, NIN),
                    ("w_att_branch", 512, D), ("w_hgrn_branch", D, D), ("w_mix_out", D, D),
                    ("xattn_wq", D, D), ("xattn_wkv", D, 2 * D), ("xattn_wo", D, D),
                    ("ffn2_w_gu", D, 2 * DFF), ("ffn2_w_down", DFF, D)]:
        W[n] = din(n, [NL, r, c], F32)
    out = nc.dram_tensor("out", [T, D], F32, kind="ExternalOutput").ap()
    xscr = nc.dram_tensor("xscr", [T, D], F32).ap()
    NWT = NL * 80
    wscr = nc.dram_tensor("wscr", [NWT, 128, 8 * 512], BF16).ap()
    ropec = nc.dram_tensor("ropec", [128, T], F32).ap()
    ropes = nc.dram_tensor("ropes", [128, T], F32).ap()
    b_xscr = Buf("xscr")
    b_rope = Buf("ropescr")

    st = ExitStack()
    with st:
        def sb(name, shape, dt):
            return st.enter_context(nc.sbuf_tensor("sb_" + name, shape, dt))

        cst = {}
        b_const = Buf("const")
        for n, sh, dt in CONST_SPECS:
            cst[n] = sb("s" + n, sh, dt)
        vecs = sb("vecs", [128, NL * VC], F32)
        fing = sb("fing", [128, D], F32)
        epsc = sb("epsc", [128, 1], F32)
        lbc = sb("lbc", [128, NL * 8], F32)
        omlc = sb("omlc", [128, NL * 8], F32)
        x_sb = sb("x_sb", [128, NS, D], F32)
        b_x = [Buf(f"x{s}") for s in range(NS)]
        xn = [sb(f"xn{s}", [128, D], BF16) for s in range(NS)]
        b_xn = [Buf(f"xn{s}") for s in range(NS)]
        b_junk = Buf("junk")
        ss = sb("ss", [128, 4], F32)
        rs = sb("rs", [128, 4], F32)
        rstd = sb("rstd", [128, 4], F32)
        b_ss, b_rs, b_rstd = Buf("ss"), Buf("rs"), Buf("rstd")
        hT = sb("hT", [128, 8, TT], BF16)
        b_hT = Buf("hT")
        NSLOT = 4
        wsl = [sb(f"w{i}", [128, 8, 512], BF16) for i in range(NSLOT)]
        b_w = [Buf(f"w{i}") for i in range(NSLOT)]
        kc = [sb(f"kc{g}", [128, NBLK[g], 4, 128], BF16) for g in range(3)]
        vc = [sb(f"vc{g}", [128, NBLK[g], 4, 128], BF16) for g in range(3)]
        b_kc = [[Buf(f"kc{g}_{i}") for i in range(NBLK[g])] for g in range(3)]
        b_vc = [[Buf(f"vc{g}_{i}") for i in range(NBLK[g])] for g in range(3)]
        Sst = sb("Sst", [128, 8, 128], F32)
        Stb = sb("Stb", [128, 8, 128], BF16)
        b_S = [Buf(f"S{h}") for h in range(8)]
        b_St = [Buf(f"St{h}") for h in range(8)]
        kxT = sb("kxT", [128, 8, NMEM], BF16)
        vx = sb("vx", [128, 2, D], BF16)
        b_kxT, b_vx = Buf("kxT"), Buf("vx")
        ropc = sb("ropc", [128, TT], F32)
        rops = sb("rops", [128, TT], F32)
        b_ropt = Buf("ropt")
        act = sb("act", [128, 22, TT], BF16)
        b_act = Buf("act")
        qa = sb("qa", [128, 12, TT], BF16)
        junk = qa[:, 0:4, :].rearrange("p a b -> p (a b)")
        b_qa = [Buf(f"qa{i}") for i in range(12)]
        raw = [sb(f"raw{i}", [128, TT], BF16) for i in range(2)]
        b_raw = [Buf(f"raw{i}") for i in range(2)]
        rt1 = [sb(f"rt1_{i}", [128, TT], F32) for i in range(2)]
        rt2 = [sb(f"rt2_{i}", [128, TT], F32) for i in range(2)]
        b_rt1 = [Buf(f"rt1_{i}") for i in range(2)]
        b_rt2 = [Buf(f"rt2_{i}") for i in range(2)]
        pbufs = [sb(f"pb{i}", [128, 512], BF16) for i in range(4)]
        b_pbufs = [Buf(f"pbuf{i}") for i in range(4)]
        rec = [sb(f"rec{i}", [128, TT], F32) for i in range(2)]
        b_rec = [Buf(f"rec{i}") for i in range(2)]
        oa = sb("oa", [128, 4, TT], BF16)
        b_oa = Buf("oa")
        Tt = [sb(f"T{i}", [128, TT], F32) for i in range(9)]
        b_T = [Buf(f"T{i}") for i in range(9)]
        klb = sb("klb", [128, TT], BF16)
        b_klb = Buf("klb")
        qd = sb("qd", [128, 8, TT], BF16)
        kd = sb("kd", [128, 8, TT], BF16)
        b_qd = [Buf(f"qd{h}") for h in range(8)]
        b_kd = [Buf(f"kd{h}") for h in range(8)]
        klT = sb("klT", [128, 8, NS, 128], BF16)
        b_klT = [Buf(f"klT{h}") for h in range(8)]
        dec = sb("dec", [128, 8, 4], F32)
        emid = sb("emid", [128, 8, 4], F32)
        el = sb("el", [128, 8, 4], F32)
        b_dec = [Buf(f"dec{h}") for h in range(8)]
        iTM = sb("iTM", [128, NS, D], BF16)
        b_iTM = Buf("iTM")
        ATb = [sb(f"AT{i}", [128, 128], BF16) for i in range(4)]
        b_AT = [Buf(f"AT{i}") for i in range(4)]
        oT = sb("oT", [128, 8, TT], F32)
        b_oT = [Buf(f"oT{h}") for h in range(8)]
        sqb = sb("sqb", [128, TT], BF16)
        b_sqb = Buf("sqb")
        ob = sb("ob", [128, 8, TT], BF16)
        b_ob = Buf("ob")
        ya = sb("ya", [128, TT], F32)
        yb = sb("yb", [128, TT], F32)
        b_ya, b_yb = Buf("ya"), Buf("yb")
        merged = sb("merged", [128, 8, TT], BF16)
        b_merged = Buf("merged")
        qx = qd
        b_qx = Buf("qx")
        pT = sb("pT", [128, 2, TT], BF16)
        b_pT = Buf("pT")
        ox = kd
        b_ox = Buf("ox")
        o_sb = act[:, 0:16, :].rearrange("p a b -> p (a b)").bitcast(F32).rearrange("p (s d) -> p s d", d=D)
        b_o = [b_act, b_act]
        NB = 6
        pbank = [st.enter_context(nc.psum_tensor(f"pb{i}", [128, 512], F32)) for i in range(NB)]
        b_pb = [Buf(f"pb{i}") for i in range(NB)]
        ptb = [st.enter_context(nc.psum_tensor(f"ptb{i}", [128, 1024], BF16)) for i in range(2)]
        b_ptb = [Buf(f"ptb{i}") for i in range(2)]
        ctr = {"bank": 0, "tb": 0, "w": 0, "raw": 0, "pe": 0, "rec": 0, "AT": 0}

        held = set()

        def bank(hold=False):
            while True:
                i = ctr["bank"] % NB
                ctr["bank"] += 1
                if i not in held:
                    break
            if hold:
                held.add(i)
            return pbank[i], b_pb[i]

        def release(bb):
            held.discard(b_pb.index(bb))

        def tbank():
            i = ctr["tb"] % 2
            ctr["tb"] += 1
            return ptb[i], b_ptb[i]

        def rot(name, n):
            i = ctr[name] % n
            ctr[name] += 1
            return i

        def mm(o, l, r, start, stop, reads, writes):
            S.op(PE, lambda e: e.matmul(o, lhsT=l, rhs=r, start=start, stop=stop), reads=reads, writes=writes, partial=True)

        def tr(o, i, reads, writes):
            idt = cst["c_ident"]
            S.op(PE, lambda e: e.transpose(o, i, idt[:, :]), reads=reads, writes=writes, partial=True)

        def act_(o, i, func, reads, writes, scale=1.0, bias=None, accum=None, partial=False):
            kw = {}
            if bias is not None:
                kw["bias"] = bias
            if accum is not None:
                kw["accum_out"] = accum
            S.op(ACT, lambda e: e.activation(out=o, in_=i, func=func, scale=scale, **kw), reads=reads, writes=writes, partial=partial)

        def tt(o, a, b, op, reads, writes, partial=False, eng=DVE):
            S.op(eng, lambda e: e.tensor_tensor(out=o, in0=a, in1=b, op=op), reads=reads, writes=writes, partial=partial)

        def ts(o, a, s1, s2, op0, op1, reads, writes, partial=False):
            if s2 is None:
                S.op(DVE, lambda e: e.tensor_scalar(out=o, in0=a, scalar1=s1, scalar2=None, op0=op0), reads=reads, writes=writes, partial=partial)
            else:
                S.op(DVE, lambda e: e.tensor_scalar(out=o, in0=a, scalar1=s1, scalar2=s2, op0=op0, op1=op1), reads=reads, writes=writes, partial=partial)

        def stt(o, a, sc, b, op0, op1, reads, writes, partial=False):
            S.op(DVE, lambda e: e.scalar_tensor_tensor(out=o, in0=a, scalar=sc, in1=b, op0=op0, op1=op1), reads=reads, writes=writes, partial=partial)

        wt_index = {}

        def wtile(name, l, r0, nk, c0):
            i = rot("w", NSLOT)
            dst = wsl[i][:, 0:nk, :]
            key = (name, l, r0, c0)
            if key not in wt_index:
                idx = len(wt_index)
                assert idx < NWT
                wt_index[key] = (idx, Buf(f"wscr{idx}"))
                bscr = wt_index[key][1]
                src = W[name][l][r0:r0 + nk * 128, c0:c0 + 512].rearrange("(k p) c -> p k c", p=128)
                S.dma(POOL, lambda e: e.dma_start(out=dst, in_=src), writes=[b_w[i]], partial=False)
                scr = wscr[idx].rearrange("p (k c) -> p k c", c=512)[:, 0:nk, :]
                S.dma(SP, lambda e: e.dma_start(out=scr, in_=dst), reads=[b_w[i]], writes=[bscr], key=b_w[i])
            else:
                idx, bscr = wt_index[key]
                scr = wscr[idx].rearrange("p (k c) -> p k c", c=512)[:, 0:nk, :]
                S.dma(SP, lambda e: e.dma_start(out=dst, in_=scr), reads=[bscr], writes=[b_w[i]], partial=False, key=b_w[i])
            return wsl[i], b_w[i]

        for n, sh, dt in CONST_SPECS:
            t_ = cst[n]
            full = t_[:, :, :] if len(sh) == 3 else t_[:, :]
            srcf = dr[n][:, :, :] if len(sh) == 3 else dr[n][:, :]
            S.dma(SP, lambda e, o=full, i=srcf: e.dma_start(out=o, in_=i), writes=[b_const])
        S.dma(SP, lambda e: e.dma_start(out=vecs[:, :], in_=vecs_in[:, :]), writes=[b_const])
        S.dma(SP, lambda e: e.dma_start(out=fing[:, :], in_=fin_in[:, :]), writes=[b_const])
        b_c2 = Buf("const2")
        S.op(DVE, lambda e: e.memset(epsc[:, :], EPS), writes=[b_c2], partial=True)
        S.op(DVE, lambda e: e.memset(lbc[:, :], 0.0), writes=[b_c2], partial=True)
        if NL == 2:
            tt(lbc[:, 8:16], vecs[:, VC + 48:VC + 56], vecs[:, 48:56], ALU.subtract, [b_const, b_c2], [b_c2])
            act_(lbc[:, 8:16], lbc[:, 8:16], AF.Sigmoid, [b_c2], [b_c2])
        ts(omlc[:, :], lbc[:, :], -1.0, 1.0, ALU.mult, ALU.add, [b_c2], [b_c2])
        for h in range(8):
            S.op(DVE, lambda e, h=h: e.memset(Sst[:, h, :], 0.0), writes=[b_S[h]])
        for g in range(3):
            for i in range(NBLK[g]):
                S.op(DVE, lambda e, g=g, i=i: e.memset(kc[g][:, i, :, :], 0.0), writes=[b_kc[g][i]])
                S.op(DVE, lambda e, g=g, i=i: e.memset(vc[g][:, i, :, :], 0.0), writes=[b_vc[g][i]])
        posi = Tt[5][:, :].bitcast(I32)
        ki = Tt[6][:, :].bitcast(I32)
        b_posi = b_T[5]
        TWO_PI = 2.0 * math.pi
        for ch in range(T // TT):
            cs = slice(ch * TT, (ch + 1) * TT)
            S.dma(SP, lambda e, cs=cs: e.dma_start(out=posi[:, :], in_=pos_in[:, cs]), writes=[b_posi], partial=False)
            A, Bq, C, Dd, E = Tt[0], Tt[1], Tt[2], Tt[3], Tt[4]
            bA, bB, bC, bD, bE = b_T[0], b_T[1], b_T[2], b_T[3], b_T[4]
            S.op(DVE, lambda e: e.tensor_copy(out=A[:, :], in_=posi[:, :]), reads=[b_posi], writes=[bA])
            fcol = cst["c_freq"]
            ts(A[:, :], A[:, :], fcol[:, 0:1], None, ALU.mult, None, [bA, b_const], [bA])
            ts(Bq[:, :], A[:, :], 1.0 / TWO_PI, None, ALU.mult, None, [bA], [bB])
            S.op(DVE, lambda e: e.tensor_copy(out=ki[:, :], in_=Bq[:, :]), reads=[bB], writes=[b_T[6]])
            S.op(DVE, lambda e: e.tensor_copy(out=Bq[:, :], in_=ki[:, :]), reads=[b_T[6]], writes=[bB])
            stt(C[:, :], Bq[:, :], -TWO_PI, A[:, :], ALU.mult, ALU.add, [bA, bB], [bC])

            def fold(R, bR, M, bM):
                ts(M[:, :], R[:, :], math.pi, None, ALU.is_gt, None, [bR], [bM])
                stt(R[:, :], M[:, :], -TWO_PI, R[:, :], ALU.mult, ALU.add, [bR, bM], [bR])
                ts(M[:, :], R[:, :], -math.pi, None, ALU.is_lt, None, [bR], [bM])
                stt(R[:, :], M[:, :], TWO_PI, R[:, :], ALU.mult, ALU.add, [bR, bM], [bR])
                ts(R[:, :], R[:, :], math.pi, -math.pi, ALU.min, ALU.max, [bR], [bR])
            fold(C, bC, Dd, bD)
            ts(E[:, :], C[:, :], math.pi / 2, None, ALU.add, None, [bC], [bE])
            fold(E, bE, Dd, bD)
            act_(C[:, :], C[:, :], AF.Sin, [bC], [bC])
            act_(E[:, :], E[:, :], AF.Sin, [bE], [bE])
            ts(C[:, :], C[:, :], fcol[:, 1:2], None, ALU.mult, None, [bC, b_const], [bC])
            S.dma(SP, lambda e, cs=cs: e.dma_start(out=ropec[:, cs], in_=E[:, :]), reads=[bE], writes=[b_rope], key=bE)
            S.dma(SP, lambda e, cs=cs: e.dma_start(out=ropes[:, cs], in_=C[:, :]), reads=[bC], writes=[b_rope], key=bC)

        def norm(src, b_src, nsub, gcol, dst, b_dst):
            for s in range(nsub):
                act_(junk[:, :], src[:, s, :], AF.Square, [b_src[s]], [b_junk, b_ss], accum=ss[:, s:s + 1], partial=True)
            act_(rs[:, 0:nsub], ss[:, 0:nsub], AF.Sqrt, [b_ss, b_c2], [b_rs], scale=1.0 / D, bias=epsc[:, 0:1])
            S.op(DVE, lambda e: e.reciprocal(out=rstd[:, 0:nsub], in_=rs[:, 0:nsub]), reads=[b_rs], writes=[b_rstd])
            for s in range(nsub):
                ts(xn[s][:, :], src[:, s, :], rstd[:, s:s + 1], None, ALU.mult, None, [b_src[s], b_rstd], [b_xn[s]])
                tb, btb = tbank()
                for c in range(8):
                    tr(tb[:, c * 128:(c + 1) * 128], xn[s][:, c * 128:(c + 1) * 128], [b_xn[s], b_const], [btb])
                gv = vecs[:, gcol:gcol + 8].unsqueeze(2).to_broadcast([128, 8, 128])
                tt(dst[:, :, s * 128:(s + 1) * 128], tb[:, :].rearrange("p (c t) -> p c t", t=128), gv, ALU.mult,
                   [btb, b_const], [b_dst], partial=True)

        def proj_fm(wt, bw, nk, jj, rhs_fn, rbufs, N):
            bk, bbk = bank()
            for k in range(nk):
                mm(bk[:, 0:N], wt[:, k, jj * 128:(jj + 1) * 128], rhs_fn(k), k == 0, k == nk - 1, [bw] + rbufs, [bbk])
            return bk, bbk

        def proj_tm_add(lhs, b_lhs, wap, l):
            for half in range(2):
                wt, bw = wtile(wap, l, 0, 8, half * 512)
                for s in range(NS):
                    bk, bbk = bank()
                    for k in range(8):
                        mm(bk[:, :], lhs[:, k, s * 128:(s + 1) * 128], wt[:, k, :], k == 0, k == 7, [b_lhs, bw], [bbk])
                    xs = x_sb[:, s, half * 512:(half + 1) * 512]
                    tt(xs, bk[:, :], xs, ALU.add, [bbk, b_x[s]], [b_x[s]])

        def ffn(l, wgu, wdn):
            for blk in range(6):
                wt, bw = wtile(wgu, l, 0, 8, blk * 512)
                for jj in range(4):
                    j = blk * 4 + jj
                    if j >= 22:
                        break
                    bk, bbk = proj_fm(wt, bw, 8, jj, lambda k: hT[:, k, :], [b_hT], TT)
                    act_(act[:, j, :], bk[:, 0:TT], AF.Silu, [bbk], [b_act], partial=True)
            for blk in range(5, 11):
                wt, bw = wtile(wgu, l, 0, 8, blk * 512)
                for jj in range(4):
                    col = blk * 512 + jj * 128
                    j = (col - DFF) // 128
                    if j < 0 or j >= 22:
                        continue
                    bk, bbk = proj_fm(wt, bw, 8, jj, lambda k: hT[:, k, :], [b_hT], TT)
                    tt(act[:, j, :], bk[:, 0:TT], act[:, j, :], ALU.mult, [bbk, b_act], [b_act], partial=True)
            for half in range(2):
                bks = [bank(hold=True) for _ in range(NS)]
                for r in range(3):
                    nk = 8 if r < 2 else 6
                    wt, bw = wtile(wdn, l, r * 1024, nk, half * 512)
                    for s in range(NS):
                        for jj in range(nk):
                            j = r * 8 + jj
                            mm(bks[s][0][:, :], act[:, j, s * 128:(s + 1) * 128], wt[:, jj, :], j == 0, j == 21,
                               [b_act, bw], [bks[s][1]])
                for s in range(NS):
                    xs = x_sb[:, s, half * 512:(half + 1) * 512]
                    stt(xs, bks[s][0][:, :], 0.5, xs, ALU.mult, ALU.add, [bks[s][1], b_x[s]], [b_x[s]])
                    release(bks[s][1])

        pend = []

        def defer(fn, depth=1):
            pend.append(fn)
            while len(pend) > depth:
                pend.pop(0)()

        def flush():
            while pend:
                pend.pop(0)()

        def rope_chunk(bk, bbk, scale, writer):
            i = rot("raw", 2)
            act_(raw[i][:, :], bk[:, 0:TT], AF.Copy, [bbk], [b_raw[i]], scale=scale)

            def stage_b(i=i):
                b2, bb2 = bank()
                pm = cst["c_perm"]
                mm(b2[:, 0:TT], pm[:, :], raw[i][:, :], True, True, [b_raw[i], b_const], [bb2])
                tt(rt1[i][:, :], raw[i][:, :], ropc[:, :], ALU.mult, [b_raw[i], b_ropt], [b_rt1[i]])
                tt(rt2[i][:, :], b2[:, 0:TT], rops[:, :], ALU.mult, [bb2, b_ropt], [b_rt2[i]])
                writer(rt1[i], rt2[i], [b_rt1[i], b_rt2[i]])
            defer(stage_b)

        def mixer(l, tau):
            t0 = tau * TT
            S.dma(SP, lambda e: e.dma_start(out=ropc[:, :], in_=ropec[:, t0:t0 + TT]), reads=[b_rope], writes=[b_ropt], partial=False)
            S.dma(SP, lambda e: e.dma_start(out=rops[:, :], in_=ropes[:, t0:t0 + TT]), reads=[b_rope], writes=[b_ropt], partial=True)
            hfn = lambda k: hT[:, k, :]
            for g in range(3):
                wt, bw = wtile("w_in", l, 0, 8, (3 + g) * 512)
                for h in range(4):
                    bk, bbk = proj_fm(wt, bw, 8, h, hfn, [b_hT], TT)

                    def wr(t1, t2, rd, g=g, h=h):
                        for s in range(NS):
                            slot = (tau * NS + s) % NBLK[g]
                            tt(kc[g][:, slot, h, :], t1[:, s * 128:(s + 1) * 128], t2[:, s * 128:(s + 1) * 128], ALU.add,
                               rd, [b_kc[g][slot]], partial=True)
                    rope_chunk(bk, bbk, 1.0, wr)
            for g in range(3):
                wt, bw = wtile("w_in", l, 0, 8, g * 512)
                for h in range(4):
                    bk, bbk = proj_fm(wt, bw, 8, h, hfn, [b_hT], TT)

                    def wr(t1, t2, rd, g=g, h=h):
                        tt(qa[:, g * 4 + h, :], t1[:, :], t2[:, :], ALU.add, rd, [b_qa[g * 4 + h]])
                    rope_chunk(bk, bbk, 128.0 ** -0.5, wr)
            for g in range(3):
                wt, bw = wtile("w_in", l, 0, 8, (6 + g) * 512)
                for h in range(4):
                    bk, bbk = proj_fm(wt, bw, 8, h, hfn, [b_hT], TT)
                    i = rot("raw", 2)
                    act_(raw[i][:, :], bk[:, 0:TT], AF.Copy, [bbk], [b_raw[i]])

                    def vstage(i=i, g=g, h=h):
                        tb, btb = tbank()
                        for s in range(NS):
                            tr(tb[:, s * 128:(s + 1) * 128], raw[i][:, s * 128:(s + 1) * 128], [b_raw[i], b_const], [btb])
                        for s in range(NS):
                            slot = (tau * NS + s) % NBLK[g]
                            S.op(ACT, lambda e, g=g, slot=slot, h=h, s=s, tb=tb: e.copy(out=vc[g][:, slot, h, :], in_=tb[:, s * 128:(s + 1) * 128]),
                                 reads=[btb], writes=[b_vc[g][slot]], partial=True)
                    defer(vstage)
            flush()
            ones = cst["c_ones"]
            msk = cst["c_mask"]
            for h in range(4):
                for i in range(NS):
                    qb = tau * NS + i
                    ents = []
                    for g in range(3):
                        grp = [(g, dl) for dl in range(DMAX[g] + 1) if qb - dl >= 0]
                        for a in range(0, len(grp), 4):
                            ents.append(grp[a:a + 4])
                    nb_, bnb = bank(hold=True)
                    db_, bdb = bank(hold=True)
                    total = sum(len(c) for c in ents)
                    cnt = 0
                    for ci, chunk in enumerate(ents):
                        sbk, bsb = bank()
                        n = len(chunk)
                        g = chunk[0][0]
                        for idx, (g_, dl) in enumerate(chunk):
                            slot = (qb - dl) % NBLK[g]
                            mm(sbk[:, idx * 128:(idx + 1) * 128], kc[g][:, slot, h, :], qa[:, g * 4 + h, i * 128:(i + 1) * 128],
                               True, True, [b_kc[g][slot], b_qa[g * 4 + h]], [bsb])
                        pi = rot("pe", 4)
                        P_ = pbufs[pi]
                        bP = b_pbufs[pi]
                        act_(P_[:, 0:n * 128], sbk[:, 0:n * 128], AF.Exp, [bsb], [bP])
                        mi = MBASE[g] + chunk[0][1]
                        tt(P_[:, 0:n * 128].rearrange("p (c t) -> p c t", t=128),
                           P_[:, 0:n * 128].rearrange("p (c t) -> p c t", t=128),
                           msk[:, mi:mi + n, :], ALU.mult, [bP, b_const], [bP])
                        last = ci == len(ents) - 1

                        def pv(chunk=chunk, g=g, P_=P_, bP=bP, cnt0=cnt, last=last, nb_=nb_, bnb=bnb, db_=db_, bdb=bdb, h=h, i=i, qb=qb, total=total):
                            c_ = cnt0
                            for idx, (g_, dl) in enumerate(chunk):
                                slot = (qb - dl) % NBLK[g]
                                mm(nb_[:, 0:128], vc[g][:, slot, h, :], P_[:, idx * 128:(idx + 1) * 128],
                                   c_ == 0, c_ == total - 1, [b_vc[g][slot], bP], [bnb])
                                mm(db_[:, 0:128], ones[:, :], P_[:, idx * 128:(idx + 1) * 128],
                                   c_ == 0, c_ == total - 1, [b_const, bP], [bdb])
                                c_ += 1
                            if last:
                                ri = rot("rec", 2)
                                S.op(DVE, lambda e, ri=ri, db_=db_: e.reciprocal(out=rec[ri][:, 0:128], in_=db_[:, 0:128]), reads=[bdb], writes=[b_rec[ri]])
                                tt(oa[:, h, i * 128:(i + 1) * 128], nb_[:, 0:128], rec[ri][:, 0:128], ALU.mult, [bnb, b_rec[ri]], [b_oa], partial=True)
                                release(bnb)
                                release(bdb)
                        cnt += n
                        defer(pv, depth=2)
            flush()
            for blk in range(2):
                wt, bw = wtile("w_in", l, 0, 8, (13 + blk) * 512)
                for s in range(NS):
                    bk, bbk = bank()
                    for k in range(8):
                        mm(bk[:, :], hT[:, k, s * 128:(s + 1) * 128], wt[:, k, :], k == 0, k == 7, [b_hT, bw], [bbk])
                    S.op(ACT, lambda e, s=s, blk=blk, bk=bk: e.copy(out=iTM[:, s, blk * 512:(blk + 1) * 512], in_=bk[:, :]),
                         reads=[bbk], writes=[b_iTM], partial=True)
            NCH = TT // 64
            csc = cst["c_scan"]
            for blk in range(2):
                wf, bwf = wtile("w_in", l, 0, 8, (11 + blk) * 512)
                wq_, bwq = wtile("w_in", l, 0, 8, (9 + blk) * 512)
                for hh in range(4):
                    h = blk * 4 + hh
                    fp, bfp = proj_fm(wf, bwf, 8, hh, hfn, [b_hT], TT)
                    T1, T2, T3, T4, T5, T6, T7, T8 = [Tt[i] for i in range(8)]
                    B1, B2, B3, B4, B5, B6, B7, B8 = [b_T[i] for i in range(8)]
                    lcol = l * 8 + h
                    act_(T1[:, :], fp[:, 0:TT], AF.Sigmoid, [bfp], [B1])
                    ts(T2[:, :], T1[:, :], omlc[:, lcol:lcol + 1], lbc[:, lcol:lcol + 1], ALU.mult, ALU.add, [B1, b_c2], [B2])
                    act_(T3[:, :], T2[:, :], AF.Ln, [B2], [B3])
                    ts(T4[:, :], T2[:, :], -1.0, 1.0, ALU.mult, ALU.add, [B2], [B4])
                    S.op(DVE, lambda e, T5=T5, T3=T3: e.tensor_tensor_scan(out=T5[:, :], data0=csc[:, :], data1=T3[:, :], initial=0.0,
                                                                             op0=ALU.mult, op1=ALU.add), reads=[B3, b_const], writes=[B5])
                    v5 = T5[:, :].rearrange("p (c t) -> p c t", t=64)
                    bmid = T5[:, 31::64].unsqueeze(2).to_broadcast([128, NCH, 64])
                    tt(T6[:, :].rearrange("p (c t) -> p c t", t=64), v5, bmid, ALU.subtract, [B5], [B6])
                    act_(T7[:, :], T6[:, :], AF.Exp, [B6], [B7])
                    act_(T8[:, :], T6[:, :], AF.Exp, [B6], [B8], scale=-1.0)
                    act_(dec[:, h, :], T5[:, 63::64], AF.Exp, [B5], [b_dec[h]], partial=True)
                    act_(emid[:, h, :], T5[:, 31::64], AF.Exp, [B5], [b_dec[h]], partial=True)
                    act_(el[:, h, :], T6[:, 63::64], AF.Exp, [B6], [b_dec[h]], partial=True)
                    tt(kd[:, h, :], T4[:, :], T8[:, :], ALU.mult, [B4, B8], [b_kd[h]])
                    elb = el[:, h, :].unsqueeze(2).to_broadcast([128, NCH, 64])
                    tt(klb[:, :].rearrange("p (c t) -> p c t", t=64), kd[:, h, :].rearrange("p (c t) -> p c t", t=64), elb, ALU.mult,
                       [b_kd[h], b_dec[h]], [b_klb])
                    tb, btb = tbank()
                    for s in range(NS):
                        tr(tb[:, s * 128:(s + 1) * 128], klb[:, s * 128:(s + 1) * 128], [b_klb, b_const], [btb])
                    S.op(ACT, lambda e, h=h, tb=tb: e.copy(out=klT[:, h, :, :], in_=tb[:, 0:NS * 128].rearrange("p (s n) -> p s n", n=128)),
                         reads=[btb], writes=[b_klT[h]])
                    qp, bqp = proj_fm(wq_, bwq, 8, hh, hfn, [b_hT], TT)
                    stt(qd[:, h, :], qp[:, 0:TT], 128.0 ** -0.5, T7[:, :], ALU.mult, ALU.mult, [bqp, B7], [b_qd[h]])
            caus = cst["c_caus"]
            for s in range(NS):
                for grp in range(2):
                    hs = list(range(grp * 4, grp * 4 + 4))
                    obks = {}
                    for h in hs:
                        ab, bab = bank()
                        mm(ab[:, 0:128], kd[:, h, s * 128:(s + 1) * 128], qd[:, h, s * 128:(s + 1) * 128], True, True, [b_kd[h], b_qd[h]], [bab])
                        ai = h % 4
                        tt(ATb[ai][:, :], ab[:, 0:128], caus[:, :], ALU.mult, [bab, b_const], [b_AT[ai]])
                    for h in hs:
                        ts(Stb[:, h, :], Sst[:, h, :], emid[:, h, 2 * s:2 * s + 1], None, ALU.mult, None, [b_S[h], b_dec[h]], [b_St[h]])
                    for h in hs:
                        obk, bob = bank(hold=True)
                        obks[h] = (obk, bob)
                        mm(obk[:, 0:128], iTM[:, s, h * 128:(h + 1) * 128], ATb[h % 4][:, :], True, False, [b_iTM, b_AT[h % 4]], [bob])
                    for cc in range(2):
                        c = 2 * s + cc
                        for h in hs:
                            obk, bob = obks[h]
                            if cc > 0:
                                ts(Stb[:, h, :], Sst[:, h, :], emid[:, h, c:c + 1], None, ALU.mult, None, [b_S[h], b_dec[h]], [b_St[h]])
                            mm(obk[:, cc * 64:(cc + 1) * 64], Stb[:, h, :], qd[:, h, c * 64:(c + 1) * 64], False, cc == 1, [b_St[h], b_qd[h]], [bob])
                            sbk, bsb = bank()
                            mm(sbk[:, 0:128], klT[cc * 64:(cc + 1) * 64, h, s, :], iTM[cc * 64:(cc + 1) * 64, s, h * 128:(h + 1) * 128], True, True,
                               [b_klT[h], b_iTM], [bsb])
                            stt(Sst[:, h, :], Sst[:, h, :], dec[:, h, c:c + 1], sbk[:, 0:128], ALU.mult, ALU.add, [b_S[h], b_dec[h], bsb], [b_S[h]])
                    for h in hs:
                        obk, bob = obks[h]
                        S.op(ACT, lambda e, h=h, s=s, obk=obk: e.copy(out=oT[:, h, s * 128:(s + 1) * 128], in_=obk[:, 0:128]),
                             reads=[bob], writes=[b_oT[h]], partial=True)
                        release(bob)
            for blk in range(2):
                wo_, bwo = wtile("w_in", l, 0, 8, (15 + blk) * 512)
                for hh in range(4):
                    h = blk * 4 + hh
                    act_(sqb[:, :], oT[:, h, :], AF.Square, [b_oT[h]], [b_sqb])
                    bk, bbk = bank()
                    mm(bk[:, 0:TT], ones[:, :], sqb[:, :], True, True, [b_sqb, b_const], [bbk])
                    act_(Tt[0][:, :], bk[:, 0:TT], AF.Sqrt, [bbk, b_c2], [b_T[0]], scale=1.0 / 128, bias=epsc[:, 0:1])
                    S.op(DVE, lambda e: e.reciprocal(out=Tt[1][:, :], in_=Tt[0][:, :]), reads=[b_T[0]], writes=[b_T[1]])
                    tt(Tt[2][:, :], oT[:, h, :], Tt[1][:, :], ALU.mult, [b_oT[h], b_T[1]], [b_T[2]])
                    gp, bgp = proj_fm(wo_, bwo, 8, hh, hfn, [b_hT], TT)
                    act_(Tt[3][:, :], gp[:, 0:TT], AF.Silu, [bgp], [b_T[3]])
                    hn = l * VC + 40 + h
                    stt(ob[:, h, :], Tt[2][:, :], vecs[:, hn:hn + 1], Tt[3][:, :], ALU.mult, ALU.mult, [b_T[2], b_T[3], b_const], [b_ob], partial=True)
            for blk in range(2):
                wa, bwa = wtile("w_att_branch", l, 0, 4, blk * 512)
                wb_, bwb = wtile("w_hgrn_branch", l, 0, 8, blk * 512)
                wga, bwga = wtile("w_in", l, 0, 8, (17 + blk) * 512)
                wgb, bwgb = wtile("w_in", l, 0, 8, (19 + blk) * 512)
                for jj in range(4):
                    c = blk * 4 + jj
                    pa, bpa = proj_fm(wa, bwa, 4, jj, lambda k: oa[:, k, :], [b_oa], TT)
                    pga, bpga = proj_fm(wga, bwga, 8, jj, hfn, [b_hT], TT)
                    act_(Tt[4][:, :], pga[:, 0:TT], AF.Sigmoid, [bpga], [b_T[4]])
                    tt(ya[:, :], pa[:, 0:TT], Tt[4][:, :], ALU.mult, [bpa, b_T[4]], [b_ya])
                    pb_, bpb = proj_fm(wb_, bwb, 8, jj, lambda k: ob[:, k, :], [b_ob], TT)
                    pgb, bpgb = proj_fm(wgb, bwgb, 8, jj, hfn, [b_hT], TT)
                    act_(Tt[5][:, :], pgb[:, 0:TT], AF.Sigmoid, [bpgb], [b_T[5]])
                    tt(yb[:, :], pb_[:, 0:TT], Tt[5][:, :], ALU.mult, [bpb, b_T[5]], [b_yb])
                    tt(merged[:, c, :], ya[:, :], yb[:, :], ALU.add, [b_ya, b_yb], [b_merged], partial=True)
            proj_tm_add(merged, b_merged, "w_mix_out", l)

        def xattn(l, tau):
            if tau == 0:
                S.dma(SP, lambda e: e.dma_start(out=o_sb[:, :, :], in_=mem_in.rearrange("(s p) d -> p s d", p=128)), writes=b_o, partial=False)
                norm(o_sb, b_o, 2, l * VC + 24, qx, b_qx)
                for blk in range(2):
                    wt, bw = wtile("xattn_wkv", l, 0, 8, blk * 512)
                    for jj in range(4):
                        bk, bbk = proj_fm(wt, bw, 8, jj, lambda k: qx[:, k, :], [b_qx], NMEM)
                        S.op(ACT, lambda e, blk=blk, jj=jj, bk=bk: e.copy(out=kxT[:, blk * 4 + jj, :], in_=bk[:, 0:NMEM]),
                             reads=[bbk], writes=[b_kxT], partial=True)
                for blk in range(2):
                    wt, bw = wtile("xattn_wkv", l, 0, 8, 1024 + blk * 512)
                    for s in range(2):
                        bk, bbk = bank()
                        for k in range(8):
                            mm(bk[:, :], qx[:, k, s * 128:(s + 1) * 128], wt[:, k, :], k == 0, k == 7, [b_qx, bw], [bbk])
                        S.op(ACT, lambda e, s=s, blk=blk, bk=bk: e.copy(out=vx[:, s, blk * 512:(blk + 1) * 512], in_=bk[:, :]),
                             reads=[bbk], writes=[b_vx], partial=True)
            norm(x_sb, b_x, NS, l * VC + 16, hT, b_hT)
            for blk in range(2):
                wt, bw = wtile("xattn_wq", l, 0, 8, blk * 512)
                for jj in range(4):
                    bk, bbk = proj_fm(wt, bw, 8, jj, lambda k: hT[:, k, :], [b_hT], TT)
                    S.op(ACT, lambda e, blk=blk, jj=jj, bk=bk: e.activation(out=qx[:, blk * 4 + jj, :], in_=bk[:, 0:TT], func=AF.Copy, scale=256.0 ** -0.5),
                         reads=[bbk], writes=[b_qx], partial=True)
            ones = cst["c_ones"]
            for h in range(4):
                for mc in range(2):
                    sbk, bsb = bank()
                    for dh in range(2):
                        mm(sbk[:, 0:TT], kxT[:, h * 2 + dh, mc * 128:(mc + 1) * 128], qx[:, h * 2 + dh, :], dh == 0, dh == 1, [b_kxT, b_qx], [bsb])
                    act_(pT[:, mc, :], sbk[:, 0:TT], AF.Exp, [bsb], [b_pT], partial=True)
                db_, bdb = bank()
                for mc in range(2):
                    mm(db_[:, 0:TT], ones[:, :], pT[:, mc, :], mc == 0, mc == 1, [b_const, b_pT], [bdb])
                ri = rot("rec", 2)
                S.op(DVE, lambda e, ri=ri, db_=db_: e.reciprocal(out=rec[ri][:, :], in_=db_[:, 0:TT]), reads=[bdb], writes=[b_rec[ri]])
                for dh in range(2):
                    obk, bob = bank()
                    for mc in range(2):
                        mm(obk[:, 0:TT], vx[:, mc, h * 256 + dh * 128:h * 256 + (dh + 1) * 128], pT[:, mc, :], mc == 0, mc == 1, [b_vx, b_pT], [bob])
                    tt(ox[:, h * 2 + dh, :], obk[:, 0:TT], rec[ri][:, :], ALU.mult, [bob, b_rec[ri]], [b_ox], partial=True)
            proj_tm_add(ox, b_ox, "xattn_wo", l)

        out_ops = []
        for l in range(NL):
            for tau in range(NT):
                t0 = tau * TT
                src = x_in if l == 0 else xscr
                rd = [] if l == 0 else [b_xscr]
                S.dma(SP, lambda e, src=src, t0=t0: e.dma_start(out=x_sb[:, :, :], in_=src[t0:t0 + TT, :].rearrange("(s p) d -> p s d", p=128)),
                      reads=rd, writes=b_x, partial=False)
                norm(x_sb, b_x, NS, l * VC + 0, hT, b_hT)
                ffn(l, "ffn1_w_gu", "ffn1_w_down")
                norm(x_sb, b_x, NS, l * VC + 8, hT, b_hT)
                mixer(l, tau)
                xattn(l, tau)
                norm(x_sb, b_x, NS, l * VC + 32, hT, b_hT)
                ffn(l, "ffn2_w_gu", "ffn2_w_down")
                if l == NL - 1:
                    for s in range(NS):
                        act_(junk[:, :], x_sb[:, s, :], AF.Square, [b_x[s]], [b_junk, b_ss], accum=ss[:, s:s + 1], partial=True)
                    act_(rs[:, 0:NS], ss[:, 0:NS], AF.Sqrt, [b_ss, b_c2], [b_rs], scale=1.0 / D, bias=epsc[:, 0:1])
                    S.op(DVE, lambda e: e.reciprocal(out=rstd[:, 0:NS], in_=rs[:, 0:NS]), reads=[b_rs], writes=[b_rstd])
                    for s in range(NS):
                        stt(o_sb[:, s, :], x_sb[:, s, :], rstd[:, s:s + 1], fing[:, :], ALU.mult, ALU.mult, [b_x[s], b_rstd, b_const], [b_o[s]])
                    o = S.dma(SP, lambda e, t0=t0: e.dma_start(out=out[t0:t0 + TT, :].rearrange("(s p) d -> p s d", p=128), in_=o_sb[:, :, :]),
                              reads=b_o, key=b_o[0])
                    out_ops.append(o)
                else:
                    o = S.dma(SP, lambda e, t0=t0: e.dma_start(out=xscr[t0:t0 + TT, :].rearrange("(s p) d -> p s d", p=128), in_=x_sb[:, :, :]),
                              reads=b_x, writes=[b_xscr], key=b_x[0])
            if l < NL - 1:
                for h in range(8):
                    S.op(DVE, lambda e, h=h: e.memset(Sst[:, h, :], 0.0), writes=[b_S[h]])

        S.finalize(nc, st)
        blk_ = st.enter_context(nc.Block())

        @blk_.tensor
        def _(e):
            S.emit_engine(PE, e)

        @blk_.scalar
        def _(e):
            S.emit_engine(ACT, e)

        @blk_.vector
        def _(e):
            S.emit_engine(DVE, e)

        @blk_.gpsimd
        def _(e):
            S.emit_engine(POOL, e)

        @blk_.sync
        def _(e):
            S.emit_engine(SP, e)
            for o in out_ops:
                e.wait_ge(o.ev[0], o.ev[1])
    return nc, S


def _layout_inputs(inp, b, T, NL):
    m = {}
    m["x"] = np.ascontiguousarray(inp["x"][b, :T])
    m["mem"] = np.ascontiguousarray(inp["mem"][b])
    m["posb"] = np.ascontiguousarray(np.broadcast_to(np.asarray(inp["positions"])[b, :T].astype(np.int32)[None, :], (128, T)))
    cols = []
    for l in range(NL):
        for n in ["ffn1_norm", "mix_norm", "xattn_norm", "mem_norm", "ffn2_norm", "hgrn_head_norm", "hgrn_lower_bounds"]:
            cols.append(np.asarray(inp[n])[l].reshape(8, 128).T)
    m["vecs"] = np.ascontiguousarray(np.concatenate(cols, axis=1).astype(np.float32))
    m["fing"] = np.ascontiguousarray(np.broadcast_to(np.asarray(inp["final_norm"]).astype(np.float32)[None, :], (128, D)))
    m.update(_host_consts())
    for n in ["ffn1_w_gu", "ffn1_w_down", "w_in", "w_att_branch", "w_hgrn_branch", "w_mix_out",
              "xattn_wq", "xattn_wkv", "xattn_wo", "ffn2_w_gu", "ffn2_w_down"]:
        m[n] = np.ascontiguousarray(np.asarray(inp[n])[:NL])
    return m


_CACHE = {}


def kernel(**inputs):
    inp = {k: np.asarray(v) for k, v in inputs.items()}
    B, T, _ = inp["x"].shape
    NL = inp["w_in"].shape[0]
    key = (T, NL)
    if key not in _CACHE:
        _CACHE[key] = build_program(T, NL)[0]
    nc = _CACHE[key]
    in_maps = [_layout_inputs(inp, b, T, NL) for b in range(B)]
    res = run_bass_kernel_spmd(nc, in_maps, core_ids=list(range(B)))
    return np.stack([np.asarray(r["out"]).reshape(T, D) for r in res.results], axis=0).astype(np.float32)
```
